# Optimizing a Trainium2 kernel written in Bass

```python
import jax, jax.numpy as jnp
from jax import lax
import numpy as np

D_MODEL = 1024
BATCH = 4
SEQ = 8192
DEPTH = 4

GRID_W = 64
CTX_LEN = 256
A_WIDTH = 512
B_WIDTH = 512
A_CONV = 3
B_CONV = 31
AB_IN = 3 * A_WIDTH + 2 * B_WIDTH
AB_OUT = A_WIDTH + B_WIDTH
N_HEADS = 16
N_KV_HEADS = 4
HEAD_DIM = 64
Q_GROUP = N_HEADS // N_KV_HEADS
Q_W = N_HEADS * HEAD_DIM
KV_W = N_KV_HEADS * HEAD_DIM
WINDOW = 128
BLOCK = 128
ROPE_THETA = 10000.0
D_FF = 2816
FFN_CONV = 3
EPS = 1e-6
NEG_INF = -1e30
N_EVEN = (DEPTH + 1) // 2
N_ODD = DEPTH // 2

kernel_name = 'hybrid_conv_swa_diffusion_block'


def rmsnorm(x, g):
    xf = x.astype(jnp.float32)
    y = xf * lax.rsqrt(jnp.mean(xf * xf, axis=-1, keepdims=True) + EPS)
    return (y * g.astype(jnp.float32)).astype(x.dtype)


def layernorm(x, g, b):
    xf = x.astype(jnp.float32)
    mu = jnp.mean(xf, axis=-1, keepdims=True)
    var = jnp.mean(jnp.square(xf - mu), axis=-1, keepdims=True)
    y = (xf - mu) * lax.rsqrt(var + EPS)
    return (y * g.astype(jnp.float32) + b.astype(jnp.float32)).astype(x.dtype)


def modulate(x, shift, scale):
    return x * (1.0 + scale) + shift


def adaln(cond, w_mod, b_mod):
    m = jax.nn.silu(cond) @ w_mod + b_mod
    return jnp.split(m, 6, axis=-1)


def dwconv(x, w):
    k = w.shape[0]
    return lax.conv_general_dilated(
        x, w[:, None, :], window_strides=(1,), padding=[(k // 2, k // 2)],
        dimension_numbers=('NWC', 'WIO', 'NWC'), feature_group_count=x.shape[-1])


def axial_rope_tables(length):
    rows = length // GRID_W
    row = jnp.repeat(jnp.arange(rows), GRID_W).astype(jnp.float32)
    col = jnp.tile(jnp.arange(GRID_W), rows).astype(jnp.float32)
    n_freq = HEAD_DIM // 4
    inv_freq = ROPE_THETA ** (-jnp.arange(n_freq, dtype=jnp.float32) / n_freq)
    ang = jnp.concatenate([row[:, None] * inv_freq, col[:, None] * inv_freq], axis=-1)
    return jnp.cos(ang)[:, None, :], jnp.sin(ang)[:, None, :]


def apply_rope(x, cos, sin):
    xf = x.astype(jnp.float32)
    x1, x2 = jnp.split(xf, 2, axis=-1)
    return jnp.concatenate([x1 * cos - x2 * sin, x2 * cos + x1 * sin], axis=-1).astype(x.dtype)


def sink_softmax(logits, sink):
    full = jnp.concatenate([logits, jnp.broadcast_to(sink, logits.shape[:-1] + (1,))], axis=-1)
    return jax.nn.softmax(full, axis=-1)[..., :-1]


def conv_mixers(h, w_in, conv_a, conv_b, conv_b_bias, ln_g, ln_b, w_out):
    p = h @ w_in
    g_b, g_c, u_a, v_b, gate_b = jnp.split(
        p, [A_WIDTH, 2 * A_WIDTH, 3 * A_WIDTH, 3 * A_WIDTH + B_WIDTH], axis=-1)
    y_a = g_b * dwconv(g_c * u_a, conv_a)
    u = v_b * jax.nn.sigmoid(gate_b)
    u = dwconv(u, conv_b) + conv_b_bias
    y_b = jax.nn.silu(layernorm(u, ln_g, ln_b))
    return jnp.concatenate([y_a, y_b], axis=-1) @ w_out


def windowed_gqa(h, hc, w_qkv, w_o, sinks, need_ctx_out):
    bsz, length, _ = h.shape
    n_ctx = hc.shape[1]
    scale = HEAD_DIM ** -0.5
    q, k, v = jnp.split(h @ w_qkv, [Q_W, Q_W + KV_W], axis=-1)
    q = q.reshape(bsz, length, N_HEADS, HEAD_DIM)
    k = k.reshape(bsz, length, N_KV_HEADS, HEAD_DIM)
    v = v.reshape(bsz, length, N_KV_HEADS, HEAD_DIM)
    cos, sin = axial_rope_tables(length)
    q = apply_rope(q, cos, sin) * scale
    k = apply_rope(k, cos, sin)
    kc, vc = jnp.split(hc @ w_qkv[:, Q_W:], [KV_W], axis=-1)
    kc = kc.reshape(bsz, n_ctx, N_KV_HEADS, HEAD_DIM)
    vc = vc.reshape(bsz, n_ctx, N_KV_HEADS, HEAD_DIM)
    sink = sinks.astype(jnp.float32).reshape(1, N_KV_HEADS, Q_GROUP, 1, 1)

    nblk = length // BLOCK
    qb = q.reshape(bsz, nblk, BLOCK, N_KV_HEADS, Q_GROUP, HEAD_DIM)

    def band(t):
        tb = t.reshape(bsz, nblk, BLOCK, N_KV_HEADS, HEAD_DIM)
        tp = jnp.pad(tb, ((0, 0), (1, 1), (0, 0), (0, 0), (0, 0)))
        return jnp.concatenate([tp[:, :-2], tp[:, 1:-1], tp[:, 2:]], axis=2)

    k_band, v_band = band(k), band(v)
    blk = jnp.arange(nblk)[:, None, None]
    q_pos = blk * BLOCK + jnp.arange(BLOCK)[None, :, None]
    k_pos = (blk - 1) * BLOCK + jnp.arange(3 * BLOCK)[None, None, :]
    mask = (jnp.abs(q_pos - k_pos) <= WINDOW) & (k_pos >= 0) & (k_pos < length)

    def attend_block(args):
        q_blk, k_blk, v_blk, m = args
        s_loc = jnp.einsum('bqhgd,bshd->bhgqs', q_blk, k_blk).astype(jnp.float32)
        s_loc = jnp.where(m, s_loc, NEG_INF)
        s_ctx = jnp.einsum('bqhgd,bchd->bhgqc', q_blk, kc).astype(jnp.float32)
        p = sink_softmax(jnp.concatenate([s_loc, s_ctx], axis=-1), sink)
        p_loc = p[..., :3 * BLOCK].astype(v_blk.dtype)
        p_ctx = p[..., 3 * BLOCK:].astype(vc.dtype)
        return (jnp.einsum('bhgqs,bshd->bqhgd', p_loc, v_blk)
                + jnp.einsum('bhgqc,bchd->bqhgd', p_ctx, vc))

    out = lax.map(attend_block, (jnp.moveaxis(qb, 1, 0), jnp.moveaxis(k_band, 1, 0),
                                 jnp.moveaxis(v_band, 1, 0), mask))
    y = jnp.moveaxis(out, 0, 1).reshape(bsz, length, Q_W) @ w_o

    yc = None
    if need_ctx_out:
        qc = (hc @ w_qkv[:, :Q_W]).reshape(bsz, n_ctx, N_KV_HEADS, Q_GROUP, HEAD_DIM) * scale
        s = jnp.einsum('bqhgd,bchd->bhgqc', qc, kc).astype(jnp.float32)
        p = sink_softmax(s, sink).astype(vc.dtype)
        yc = jnp.einsum('bhgqc,bchd->bqhgd', p, vc).reshape(bsz, n_ctx, Q_W) @ w_o
    return y, yc


def conv_ffn(h, w_up, conv, w_down):
    u = dwconv(h @ w_up, conv)
    a, g = jnp.split(u, 2, axis=-1)
    return (jax.nn.silu(g) * a) @ w_down


def setup_inputs(seed: int = 0) -> dict:
    key = jax.random.key(seed)
    ks = jax.random.split(key, 22)

    def nrm(k, shape, s):
        return jax.random.normal(k, shape, jnp.float32) * s

    return {
        'x': nrm(ks[0], (BATCH, SEQ, D_MODEL), 1.0),
        'c': nrm(ks[1], (BATCH, D_MODEL), 1.0),
        'ctx': nrm(ks[2], (BATCH, CTX_LEN, D_MODEL), 1.0),
        'c_ctx': nrm(ks[3], (D_MODEL,), 1.0),
        'w_mod': nrm(ks[4], (DEPTH, D_MODEL, 6 * D_MODEL), 0.5 * D_MODEL ** -0.5),
        'b_mod': nrm(ks[5], (DEPTH, 6 * D_MODEL), 0.02),
        'norm_mix': 1.0 + nrm(ks[6], (DEPTH, D_MODEL), 0.02),
        'norm_ffn': 1.0 + nrm(ks[7], (DEPTH, D_MODEL), 0.02),
        'w_in_ab': nrm(ks[8], (N_EVEN, D_MODEL, AB_IN), D_MODEL ** -0.5),
        'conv_a': nrm(ks[9], (N_EVEN, A_CONV, A_WIDTH), A_CONV ** -0.5),
        'conv_b': nrm(ks[10], (N_EVEN, B_CONV, B_WIDTH), B_CONV ** -0.5),
        'conv_b_bias': nrm(ks[11], (N_EVEN, B_WIDTH), 0.02),
        'ln_b_gain': 1.0 + nrm(ks[12], (N_EVEN, B_WIDTH), 0.02),
        'ln_b_bias': nrm(ks[13], (N_EVEN, B_WIDTH), 0.02),
        'w_out_ab': nrm(ks[14], (N_EVEN, AB_OUT, D_MODEL), AB_OUT ** -0.5),
        'w_qkv': nrm(ks[15], (N_ODD, D_MODEL, Q_W + 2 * KV_W), D_MODEL ** -0.5),
        'w_o': nrm(ks[16], (N_ODD, Q_W, D_MODEL), Q_W ** -0.5),
        'sinks': nrm(ks[17], (N_ODD, N_HEADS), 1.0),
        'w_up': nrm(ks[18], (DEPTH, D_MODEL, 2 * D_FF), D_MODEL ** -0.5),
        'w_conv_ffn': nrm(ks[19], (DEPTH, FFN_CONV, 2 * D_FF), FFN_CONV ** -0.5),
        'w_down': nrm(ks[20], (DEPTH, D_FF, D_MODEL), D_FF ** -0.5),
        'final_norm': 1.0 + nrm(ks[21], (D_MODEL,), 0.02),
    }


def reference(x, c, ctx, c_ctx, w_mod, b_mod, norm_mix, norm_ffn, w_in_ab, conv_a, conv_b,
              conv_b_bias, ln_b_gain, ln_b_bias, w_out_ab, w_qkv, w_o, sinks, w_up, w_conv_ffn,
              w_down, final_norm):
    xc = ctx
    for l in range(DEPTH):
        last = l == DEPTH - 1
        sh1, sc1, g1, sh2, sc2, g2 = adaln(c[:, None, :], w_mod[l], b_mod[l])
        csh1, csc1, cg1, csh2, csc2, cg2 = adaln(c_ctx, w_mod[l], b_mod[l])
        h = modulate(rmsnorm(x, norm_mix[l]), sh1, sc1)
        hc = modulate(rmsnorm(xc, norm_mix[l]), csh1, csc1)
        if l % 2 == 0:
            e = l // 2
            y = conv_mixers(h, w_in_ab[e], conv_a[e], conv_b[e], conv_b_bias[e],
                            ln_b_gain[e], ln_b_bias[e], w_out_ab[e])
            yc = None if last else conv_mixers(hc, w_in_ab[e], conv_a[e], conv_b[e], conv_b_bias[e],
                                               ln_b_gain[e], ln_b_bias[e], w_out_ab[e])
        else:
            o = l // 2
            y, yc = windowed_gqa(h, hc, w_qkv[o], w_o[o], sinks[o], not last)
        x = x + g1 * y
        x = x + g2 * conv_ffn(modulate(rmsnorm(x, norm_ffn[l]), sh2, sc2), w_up[l], w_conv_ffn[l], w_down[l])
        if not last:
            xc = xc + cg1 * yc
            xc = xc + cg2 * conv_ffn(modulate(rmsnorm(xc, norm_ffn[l]), csh2, csc2),
                                     w_up[l], w_conv_ffn[l], w_down[l])
    return rmsnorm(x, final_norm)
```

```python
import contextlib
import numpy as np
import concourse.bass as bass
import concourse.mybir as mybir
from concourse.bass_utils import run_bass_kernel_spmd

F32 = mybir.dt.float32
BF16 = mybir.dt.bfloat16
AF = mybir.ActivationFunctionType
ALU = mybir.AluOpType

ENGS = ("tensor", "vector", "scalar", "gpsimd", "sync")

D = 1024
KC = 8
TW = 512
NT = 9
T = 4480
WIDTHS = [512] * 8 + [384]
CTXW_CONV = 288
PADL = 128
XW = PADL + T
CW = PADL + TW
SEQ = 8192
HALF = 4096
NCTX = 256
DEPTH = 4
EPS = 1e-6
O_NM, O_NF, O_BM, O_CA, O_CB, O_CBB, O_LG, O_LB, O_WCF, O_SNK, NPP = 0, 8, 16, 64, 76, 200, 204, 208, 212, 344, 360
NQ = 3328


class _Op:
    __slots__ = ("eng", "fn", "deps", "is_dma", "sem", "val", "need_sig", "sig_idx", "small")

    def __init__(self, eng, fn, is_dma):
        self.eng = eng
        self.fn = fn
        self.deps = []
        self.is_dma = is_dma
        self.sem = None
        self.val = 0
        self.need_sig = False
        self.sig_idx = 0
        self.small = False


CUR_TW = [TW]


class _W:
    def __init__(self, t):
        self.t = t

    def __getitem__(self, idx):
        if not isinstance(idx, tuple):
            idx = (idx,)
        nd = len(self.t.shape)
        if len(idx) == 1 and nd > 1 and idx[0] == slice(None):
            idx = (slice(None),) * nd
        if len(idx) == nd and idx[-1] == slice(None):
            idx = idx[:-1] + (slice(0, CUR_TW[0]),)
        return self.t[idx]


class _Rec:
    def __getattr__(self, name):
        def f(*args, **kwargs):
            return (name, args, kwargs)
        return f


R = _Rec()


class Sched:
    def __init__(self, nc, n_dma_sems=24, same_engine_sync=True):
        self.nc = nc
        self.ops = {e: [] for e in ENGS}
        self.last_w = {}
        self.readers = {}
        self.n_dma_sems = n_dma_sems
        self.dma_rr = 0
        self.dma_last = [None] * n_dma_sems
        self.dma_cnt = [0] * n_dma_sems
        self.same_engine_sync = same_engine_sync
        self.pending_fence = {e: [] for e in ENGS}

    def _dep(self, op, d):
        if d is None or d is op:
            return
        if (not d.is_dma) and (not op.is_dma) and d.eng == op.eng:
            if d.eng == "tensor" or not self.same_engine_sync:
                return
            if d.eng in ("scalar", "vector") and not d.small and not op.small:
                return
        op.deps.append(d)
        if not d.is_dma:
            d.need_sig = True

    def fence(self):
        lasts = []
        for e in ENGS:
            comp = [o for o in self.ops[e] if not o.is_dma]
            if comp:
                lasts.append(comp[-1])
        for d in self.dma_last:
            if d is not None:
                lasts.append(d)
        for e in ENGS:
            self.pending_fence[e] = list(lasts)
        self.last_w = {}
        self.readers = {}

    def add(self, eng, fn, reads=(), writes=(), dma=False, small=False):
        op = _Op(eng, fn, dma)
        op.small = small
        if self.pending_fence[eng]:
            for d in self.pending_fence[eng]:
                if d.is_dma or d.eng != eng or dma:
                    op.deps.append(d)
                    if not d.is_dma:
                        d.need_sig = True
            self.pending_fence[eng] = []
        for k in reads:
            self._dep(op, self.last_w.get(k))
        for k in writes:
            self._dep(op, self.last_w.get(k))
            for r in self.readers.get(k, ()):
                if (not dma) and (not r.is_dma) and r.eng == eng:
                    continue
                self._dep(op, r)
        if dma:
            s = self.dma_rr
            self.dma_rr = (self.dma_rr + 1) % self.n_dma_sems
            prev = self.dma_last[s]
            if prev is not None:
                op.deps.append(prev)
            self.dma_cnt[s] += 16
            op.sem = s
            op.val = self.dma_cnt[s]
            self.dma_last[s] = op
        for k in reads:
            self.readers.setdefault(k, []).append(op)
        for k in writes:
            self.last_w[k] = op
            self.readers[k] = []
        self.ops[eng].append(op)
        return op

    def emit(self, final_wait_ops=()):
        nc = self.nc
        for e in ENGS:
            c = 0
            for op in self.ops[e]:
                if not op.is_dma and op.need_sig:
                    c += 1
                    op.sig_idx = c
        with contextlib.ExitStack() as st:
            esem = {e: st.enter_context(nc.semaphore("S_" + e)) for e in ENGS}
            dsem = [st.enter_context(nc.semaphore("D_%d" % i)) for i in range(self.n_dma_sems)]
            block = st.enter_context(nc.Block())

            def make(e):
                def body(eng):
                    waited = {}
                    for op in self.ops[e]:
                        for d in op.deps:
                            if d.is_dma:
                                key, sem, val = ("d", d.sem), dsem[d.sem], d.val
                            else:
                                key, sem, val = ("e", d.eng), esem[d.eng], d.sig_idx
                            if waited.get(key, 0) >= val:
                                continue
                            waited[key] = val
                            eng.wait_ge(sem, val)
                        ins = getattr(eng, op.fn[0])(*op.fn[1], **op.fn[2])
                        if op.is_dma:
                            ins.then_inc(dsem[op.sem], 16)
                        elif op.need_sig:
                            ins.then_inc(esem[e], 1)
                    if e == "sync":
                        for d in final_wait_ops:
                            eng.wait_ge(dsem[d.sem], d.val)
                return body

            block.tensor(make("tensor"))
            block.vector(make("vector"))
            block.scalar(make("scalar"))
            block.gpsimd(make("gpsimd"))
            block.sync(make("sync"))


def build_program(n_layers=DEPTH, nt=NT, dbg=False, stop_after=None, widths=None):
    nc = bass.Bass("TRN2", target_bir_lowering=False)
    dt_in = lambda name, shape: nc.dram_tensor(name, shape, F32, kind="ExternalInput").ap()
    x_in = dt_in("x_fm", [D, XW])
    c_in = dt_in("ctx_fm", [D, CW])
    cvec = dt_in("cvec", [128, 16])
    pp_in = dt_in("pp", [128, DEPTH * NPP])
    fn_in = dt_in("fnorm", [128, 8])
    ident_in = dt_in("ident", [128, 128])
    masks_in = dt_in("masks", [128, 1024])
    rope_in = dt_in("rope", [2, 128, T])
    w_mod = dt_in("w_mod", [DEPTH, D, 6 * D])
    w_in = dt_in("w_in_ab", [2, D, 2560])
    w_out = dt_in("w_out_ab", [2, D, D])
    w_qkv = dt_in("w_qkv_ext", [2, D, NQ])
    w_o = dt_in("w_o", [2, D, D])
    w_up = dt_in("w_up", [DEPTH, D, 5632])
    w_dn = dt_in("w_down", [DEPTH, 2816, D])
    out = nc.dram_tensor("out_fm", [D, HALF], F32, kind="ExternalOutput").ap()
    skind = "ExternalOutput" if dbg else "Internal"
    XB = [nc.dram_tensor("xs%d" % i, [D, XW], F32, kind=skind).ap() for i in range(3)]
    CB = [nc.dram_tensor("cs%d" % i, [D, CW], F32, kind=skind).ap() for i in range(3)]
    dkey = {}
    for i in range(3):
        dkey[id(XB[i])] = "dX%d" % i
        dkey[id(CB[i])] = "dC%d" % i
    dkey[id(x_in)] = "dXin"
    dkey[id(c_in)] = "dCin"

    S = Sched(nc)
    outer = contextlib.ExitStack()
    sbo = lambda name, shape, dt: outer.enter_context(nc.sbuf_tensor(name, shape, dt))
    PS = [_W(outer.enter_context(nc.psum_tensor("psb%d" % i, [128, TW], F32))) for i in range(8)]
    widths = list(widths if widths is not None else WIDTHS[:nt])
    starts = [sum(widths[:i]) for i in range(len(widths))]
    pool_rr = [0]

    def ps_alloc():
        b = 1 + pool_rr[0]
        pool_rr[0] = (pool_rr[0] + 1) % 5
        return b

    PP = sbo("PP", [128, DEPTH, NPP], F32)
    FN = sbo("FN", [128, 8], F32)
    MV = sbo("MV", [128, DEPTH * 6 * 8 * 2], F32)
    IDN = sbo("IDN", [128, 128], F32)
    ONESB = sbo("ONESB", [128, 128], BF16)
    MASK = sbo("MASK", [128, 2, TW], BF16)
    xt = _W(sbo("xt", [128, KC, TW], F32))
    xr = _W(sbo("xr", [128, KC, TW], F32))
    hb = [_W(sbo("h%d" % i, [128, KC, TW], BF16)) for i in range(2)]
    sq = [_W(sbo("sq%d" % i, [128, TW], BF16)) for i in range(2)]
    stdt = _W(sbo("stdt", [128, TW], F32))
    htmp = [_W(sbo("htmp%d" % i, [128, TW], F32)) for i in range(2)]

    def geom(v, conv):
        if v:
            return [(0, CTXW_CONV if conv else NCTX)]
        return list(zip(starts, widths))

    def mv(l, j, c, v):
        o = ((l * 6 + j) * 8 + c) * 2 + v
        return MV[:, o:o + 1]

    def ppc(l, col):
        return PP[:, l, col:col + 1]

    small_mode = [False]

    def A(eng, fn, r=(), w=(), dma=False, small=False):
        return S.add(eng, fn, reads=r, writes=w, dma=dma, small=(small or small_mode[0]))

    A("sync", R.dma_start(out=PP[:], in_=pp_in.rearrange("p (l n) -> p l n", l=DEPTH)), w=["PP"], dma=True)
    A("sync", R.dma_start(out=FN[:], in_=fn_in), w=["FN"], dma=True)
    A("sync", R.dma_start(out=IDN[:], in_=ident_in), w=["IDN"], dma=True)
    A("gpsimd", R.dma_start(out=MASK[:], in_=masks_in.rearrange("p (a t) -> p a t", a=2)), w=["MASK"], dma=True)
    A("vector", R.memset(ONESB[:], 1.0), w=["ONESB"])

    with contextlib.ExitStack() as ph:
        small_mode[0] = True
        sb = lambda name, shape, dt: ph.enter_context(nc.sbuf_tensor(name, shape, dt))
        ZT = sb("ZT", [128, CW], F32)
        A("vector", R.memset(ZT[:], 0.0), w=["ZT"])
        for i in range(3):
            for c in range(KC):
                A("sync", R.dma_start(out=XB[i][c * 128:(c + 1) * 128, 0:PADL], in_=ZT[:, 0:PADL]),
                  r=["ZT"], dma=True)
                A("sync", R.dma_start(out=XB[i][c * 128:(c + 1) * 128, XW - 256:XW], in_=ZT[:, 0:256]),
                  r=["ZT"], dma=True)
                A("sync", R.dma_start(out=CB[i][c * 128:(c + 1) * 128, :], in_=ZT[:, :]),
                  r=["ZT"], dma=True)
        CV = sb("CV", [128, 16], F32)
        SCV = sb("SCV", [128, 16], F32)
        WM = [sb("WM%d" % i, [128, KC, 768], BF16) for i in range(4)]
        SCVB = sb("SCVB", [128, 16], BF16)
        MRAW = sb("MRAW", [128, 96], F32)
        A("sync", R.dma_start(out=CV[:], in_=cvec), w=["CV"], dma=True)
        A("scalar", R.activation(out=SCV[:], in_=CV[:], func=AF.Silu), r=["CV"], w=["SCV"])
        A("vector", R.tensor_copy(out=SCVB[:], in_=SCV[:]), r=["SCV"], w=["SCVB"])
        piece = 0
        for l in range(n_layers):
            for fp in range(8):
                wm = WM[piece % 4]
                wk = "WM%d" % (piece % 4)
                piece += 1
                for half in range(2):
                    A("gpsimd", R.dma_start(
                        out=wm[:, half * 4:(half + 1) * 4, :],
                        in_=w_mod[l, half * 512:(half + 1) * 512, fp * 768:(fp + 1) * 768].rearrange("(k p) n -> p k n", p=128),
                        max_dma_last_dim=4096),
                      w=["%s_%d" % (wk, half)], dma=True)
                for m in range(6):
                    ch = fp * 6 + m
                    for kc in range(KC):
                        A("tensor", R.matmul(
                            PS[0][:, ch * 2:ch * 2 + 2], lhsT=wm[:, kc, m * 128:(m + 1) * 128],
                            rhs=SCVB[:, kc * 2:kc * 2 + 2], start=(kc == 0), stop=(kc == KC - 1)),
                          r=["%s_%d" % (wk, kc // 4), "SCVB"], w=["ps0"])
            for v in range(2):
                A("vector", R.tensor_tensor(
                    out=MRAW[:, v:96:2], in0=PS[0][:, v:96:2], in1=PP[:, l, O_BM:O_BM + 48], op=ALU.add),
                  r=["ps0", "PP"], w=["MRAW"])
            for v in range(2):
                def mraw(j, v=v):
                    return MRAW[:, (j * 8) * 2 + v:(j * 8 + 8) * 2:2]

                def mvs(j, l=l, v=v):
                    o = ((l * 6 + j) * 8) * 2 + v
                    return MV[:, o:o + 15:2]
                for j in (0, 2, 3, 5):
                    A("vector", R.tensor_copy(out=mvs(j), in_=mraw(j)),
                      r=["MRAW"], w=["MV"])
                for (j, ocol) in ((1, O_NM), (4, O_NF)):
                    A("vector", R.scalar_tensor_tensor(
                        out=mvs(j), in0=mraw(j), scalar=1.0, in1=PP[:, l, ocol:ocol + 8], op0=ALU.add, op1=ALU.mult),
                      r=["MRAW", "PP"], w=["MV"])
        S.fence()
        small_mode[0] = False

    def load_x(src, col0, key="xt", dst=None, tw=None):
        dst = xt if dst is None else dst
        tw = CUR_TW[0] if tw is None else tw
        for c in range(KC):
            A("sync", R.dma_start(out=dst[:, c, 0:tw], in_=src[c * 128:(c + 1) * 128, col0:col0 + tw]),
              r=[dkey[id(src)]], w=["%s%d" % (key, c)], dma=True)

    def norm_mod(l, jg, jsh, v, hslot, nvalid=TW, gain_ap=None, out_f32=None):
        h = hb[hslot]
        hk = "h%d" % hslot
        nvalid = min(nvalid, CUR_TW[0])
        for c in range(KC):
            s = sq[c % 2]
            sk = "sq%d" % (c % 2)
            A("scalar", R.activation(out=s[:], in_=xt[:, c, :], func=AF.Square), r=["xt%d" % c], w=[sk])
            A("tensor", R.matmul(PS[0][:], lhsT=ONESB[:], rhs=s[:], start=(c == 0), stop=(c == KC - 1)),
              r=[sk, "ONESB"], w=["ps0"])
        A("scalar", R.activation(out=stdt[:], in_=PS[0][:], func=AF.Ln, scale=1.0 / D, bias=EPSB[:, 0:1]),
          r=["ps0"], w=["stdt"])
        A("scalar", R.activation(out=PS[0][:], in_=stdt[:], func=AF.Exp, scale=-0.5), r=["stdt"], w=["ps0"])
        for c in range(KC):
            g = gain_ap(c) if gain_ap is not None else mv(l, jg, c, v)
            if out_f32 is not None:
                A("vector", R.scalar_tensor_tensor(
                    out=out_f32[:, c, :], in0=xt[:, c, :], scalar=g, in1=PS[0][:], op0=ALU.mult, op1=ALU.mult),
                  r=["xt%d" % c, "ps0", "MV", "FN"], w=["xr%d" % c])
                continue
            tmp = htmp[c % 2]
            tk = "htmp%d" % (c % 2)
            A("vector", R.scalar_tensor_tensor(
                out=tmp[:], in0=xt[:, c, :], scalar=g, in1=PS[0][:], op0=ALU.mult, op1=ALU.mult),
              r=["xt%d" % c, "ps0", "MV"], w=[tk])
            A("scalar", R.activation(
                out=h[:, c, 0:nvalid], in_=tmp[:, 0:nvalid], func=AF.Identity, bias=mv(l, jsh, c, v), scale=1.0),
              r=[tk, "MV"], w=[hk])
        if nvalid < CUR_TW[0]:
            A("gpsimd", R.memset(h[:, :, nvalid:CUR_TW[0]], 0.0), w=[hk])

    def load_w(dst, key, src_rows, ncols_list):
        for (k, c0, ap) in src_rows:
            n = ap.shape[1]
            A("gpsimd", R.dma_start(out=dst[:, k, c0:c0 + n], in_=ap, max_dma_last_dim=4096),
              r=["wdram"], w=["%s%d" % (key, k)], dma=True)

    def load_wc(dst, key, src2d, nk, blocks):
        for (d0, s0, n) in blocks:
            keys = ["%s_%d" % (key, c) for c in range(d0 // 128, (d0 + n + 127) // 128)]
            A("gpsimd", R.dma_start(out=dst[:, 0:nk, d0:d0 + n],
                                    in_=src2d[:, s0:s0 + n].rearrange("(k p) n -> p k n", p=128), max_dma_last_dim=4096),
              r=["wdram"], w=keys, dma=True)

    def build_diag(dst, l, col0, ntap, nchunk, key):
        for j in range(nchunk * ntap):
            A("gpsimd", R.tensor_scalar(
                out=dst[:, j, :], in0=IDN[:], scalar1=ppc(l, col0 + j), scalar2=1.0, op0=ALU.mult, op1=ALU.mult),
              r=["IDN", "PP"], w=[key])

    def conv_pe(bank, diag, j, ntap, ext, ekey, off0, dkey):
        for k in range(ntap):
            A("tensor", R.matmul(PS[bank][:], lhsT=diag[:, j * ntap + k, :], rhs=ext[:, off0 + k:off0 + k + CUR_TW[0]],
                                                 start=(k == 0), stop=(k == ntap - 1)),
              r=[ekey, dkey], w=["ps%d" % bank])

    def residual_out(l, jgate, v, W, wkey, rhs, rkey, nk, dst, col0, alloc=None):
        for oc in range(KC):
            b = (alloc or ps_alloc)()
            for k in range(nk):
                A("tensor", R.matmul(PS[b][:], lhsT=W[:, k, oc * 128:(oc + 1) * 128], rhs=rhs[:, k, :],
                                                              start=(k == 0), stop=(k == nk - 1)),
                  r=["%s_%d" % (wkey, oc), rkey], w=["ps%d" % b])
            A("vector", R.scalar_tensor_tensor(
                out=xr[:, oc, :], in0=PS[b][:], scalar=mv(l, jgate, oc, v), in1=xr[:, oc, :], op0=ALU.mult, op1=ALU.add),
              r=["ps%d" % b, "xr%d" % oc, "MV"], w=["xr%d" % oc])
            A("sync", R.dma_start(out=dst[oc * 128:(oc + 1) * 128, col0:col0 + CUR_TW[0]], in_=xr[:, oc, :]),
              r=["xr%d" % oc], w=[dkey[id(dst)]], dma=True)

    EPSB = sbo("EPSB", [128, 1], F32)
    A("vector", R.memset(EPSB[:], EPS), w=["EPSB"])

    uid = [0]
    streams = lambda l, last: ([("ctx", 1)] if not last else []) + [("main", 0)]

    for l in range(n_layers):
        last = (l == DEPTH - 1)
        even = (l % 2 == 0)
        ia, ib, ic = [(0, 1, 2), (0, 1, 2), (0, 1, 2), (0, 1, 2)][l]
        if even:
            e_ = l // 2
            with contextlib.ExitStack() as ph:
                uid[0] += 1
                sb = lambda name, shape, dt, u=uid[0]: ph.enter_context(nc.sbuf_tensor("%s_%d" % (name, u), shape, dt))
                WI = sb("WI", [128, KC, 2560], BF16)
                WO = sb("WO", [128, KC, D], BF16)
                DGA = sb("DGA", [128, 12, 128], BF16)
                DGB = sb("DGB", [128, 124, 128], BF16)
                EXT = [sb("EXT%d" % i, [128, 30 + TW], BF16) for i in range(4)]
                EXG = [sb("EXG%d" % i, [128, 30 + TW], F32) for i in range(4)]
                EXU = [sb("EXU%d" % i, [128, 30 + TW], BF16) for i in range(4)]
                U2 = [_W(sb("U2_%d" % i, [128, TW], F32)) for i in range(4)]
                U2B = [_W(sb("U2B%d" % i, [128, TW], BF16)) for i in range(2)]
                SQB = [_W(sb("SQB%d" % i, [128, TW], BF16)) for i in range(2)]
                FT = [_W(sb("FT%d" % i, [128, TW], F32)) for i in range(3)]
                YB = [_W(sb("Y0", [128, KC, TW], BF16)), _W(sb("Y1", [128, KC, TW], BF16))]
                load_wc(WI, "WI", w_in[e_], KC, [(c * 128, c * 128, 256) for c in (8, 4, 0, 10, 6, 2, 16, 12, 18, 14)])
                load_wc(WO, "WO", w_out[e_], KC, [(c, c, 256) for c in range(0, D, 256)])
                build_diag(DGA, l, O_CA, 3, 4, "DGA")
                build_diag(DGB, l, O_CB, 31, 4, "DGB")
                ft_rr = [0]

                def ft():
                    i = ft_rr[0]
                    ft_rr[0] = (i + 1) % 3
                    return FT[i], "FT%d" % i
                tix = 0
                for (sname, v) in streams(l, last):
                    SRC = ((c_in if l == 0 else CB[0]) if v else (x_in if l == 0 else XB[0]))
                    DST = CB[1] if v else XB[1]
                    tiles = geom(v, True)
                    ntile = len(tiles)
                    tbase = tix

                    def pre(j, v=v, SRC=SRC, tbase=tbase, ntile=ntile, tiles=tiles):
                        CUR_TW[0] = tiles[j][1]
                        if j == 0:
                            load_x(SRC, PADL)
                        norm_mod(l, 1, 0, v, (tbase + j) % 2, nvalid=(NCTX if v else TW))
                        if j + 1 < ntile:
                            load_x(SRC, PADL + tiles[j + 1][0], tw=tiles[j + 1][1])
                    for i in range(4):
                        A("gpsimd", R.memset(EXT[i][:, 0:30], 0.0), w=["EXT%d" % i])
                        A("gpsimd", R.memset(EXG[i][:, 0:30], 0.0), w=["EXG%d" % i])
                        A("gpsimd", R.memset(EXU[i][:, 0:30], 0.0), w=["EXU%d" % i])

                    def first(j, tbase=tbase, tiles=tiles):
                        tw = tiles[j][1]
                        CUR_TW[0] = tw
                        hs = (tbase + j) % 2
                        h = hb[hs]
                        hk = "h%d" % hs
                        Yj = YB[j % 2]
                        yk = "Y%d" % (j % 2)

                        def proj(chunk):
                            b = ps_alloc()
                            for k in range(KC):
                                A("tensor", R.matmul(
                                    PS[b][:], lhsT=WI[:, k, chunk * 128:(chunk + 1) * 128], rhs=h[:, k, :],
                                    start=(k == 0), stop=(k == KC - 1)), r=["WI_%d" % chunk, hk], w=["ps%d" % b])
                            return b

                        def a_p(i):
                            bua = proj(8 + i)
                            f, fk = ft()
                            A("scalar", R.copy(out=f[:], in_=PS[bua][:]), r=["ps%d" % bua], w=[fk])
                            bgc = proj(4 + i)
                            A("vector", R.tensor_tensor(
                                out=EXT[i][:, 30:30 + tw], in0=PS[bgc][:], in1=f[:], op=ALU.mult),
                              r=["ps%d" % bgc, fk], w=["EXT%d" % i])
                            bgb = proj(i)
                            A("scalar", R.copy(out=EXG[i][:, 30:30 + tw], in_=PS[bgb][:]),
                              r=["ps%d" % bgb], w=["EXG%d" % i])

                        def a_c(i):
                            bc = ps_alloc()
                            conv_pe(bc, DGA, i, 3, EXT[i], "EXT%d" % i, 14, "DGA")
                            A("vector", R.tensor_tensor(
                                out=Yj[:, i, :], in0=PS[bc][:], in1=EXG[i][:, 15:15 + tw], op=ALU.mult),
                              r=["ps%d" % bc, "EXG%d" % i], w=[yk])
                            A("gpsimd", R.tensor_copy(out=EXT[i][:, 0:30], in_=EXT[i][:, tw:tw + 30]),
                              r=["EXT%d" % i], w=["EXT%d" % i])
                            A("gpsimd", R.tensor_copy(out=EXG[i][:, 0:30], in_=EXG[i][:, tw:tw + 30]),
                              r=["EXG%d" % i], w=["EXG%d" % i])

                        def b_p(i):
                            bg = proj(16 + i)
                            f, fk = ft()
                            A("scalar", R.activation(out=f[:], in_=PS[bg][:], func=AF.Sigmoid),
                              r=["ps%d" % bg], w=[fk])
                            bv = proj(12 + i)
                            A("vector", R.tensor_tensor(
                                out=EXU[i][:, 30:30 + tw], in0=PS[bv][:], in1=f[:], op=ALU.mult),
                              r=["ps%d" % bv, fk], w=["EXU%d" % i])

                        def b_c(i):
                            bc = ps_alloc()
                            conv_pe(bc, DGB, i, 31, EXU[i], "EXU%d" % i, 0, "DGB")
                            A("gpsimd", R.tensor_copy(out=EXU[i][:, 0:30], in_=EXU[i][:, tw:tw + 30]),
                              r=["EXU%d" % i], w=["EXU%d" % i])
                            A("scalar", R.activation(
                                out=U2[i][:], in_=PS[bc][:], func=AF.Identity, bias=ppc(l, O_CBB + i), scale=1.0),
                              r=["ps%d" % bc, "PP"], w=["U2_%d" % i])
                            A("scalar", R.activation(
                                out=SQB[i % 2][:], in_=PS[bc][:], func=AF.Square, bias=ppc(l, O_CBB + i), scale=1.0),
                              r=["ps%d" % bc, "PP"], w=["SQB%d" % (i % 2)])
                            A("gpsimd", R.tensor_copy(out=U2B[i % 2][:], in_=U2[i][:]),
                              r=["U2_%d" % i], w=["U2B%d" % (i % 2)])

                        def b_s(i):
                            A("tensor", R.matmul(PS[6][:], lhsT=ONESB[:], rhs=U2B[i % 2][:], start=(i == 0), stop=(i == 3)),
                              r=["U2B%d" % (i % 2), "ONESB"], w=["ps6"])
                            A("tensor", R.matmul(PS[7][:], lhsT=ONESB[:], rhs=SQB[i % 2][:], start=(i == 0), stop=(i == 3)),
                              r=["SQB%d" % (i % 2), "ONESB"], w=["ps7"])
                        a_p(0)
                        a_p(1)
                        a_c(0)
                        a_p(2)
                        a_c(1)
                        a_p(3)
                        a_c(2)
                        b_p(0)
                        a_c(3)
                        b_p(1)
                        b_c(0)
                        b_p(2)
                        b_c(1)
                        b_s(0)
                        b_p(3)
                        b_c(2)
                        b_s(1)
                        b_c(3)
                        b_s(2)
                        b_s(3)
                        f, fk = ft()
                        A("vector", R.tensor_scalar(out=PS[6][:], in0=PS[6][:], scalar1=1.0 / 512, scalar2=None, op0=ALU.mult),
                          r=["ps6"], w=["ps6"])
                        A("scalar", R.activation(out=f[:], in_=PS[6][:], func=AF.Square), r=["ps6"], w=[fk])
                        A("vector", R.scalar_tensor_tensor(
                            out=f[:], in0=PS[7][:], scalar=1.0 / 512, in1=f[:], op0=ALU.mult, op1=ALU.subtract),
                          r=["ps7", fk], w=[fk])
                        A("scalar", R.activation(out=f[:], in_=f[:], func=AF.Ln, scale=1.0, bias=EPSB[:, 0:1]),
                          r=[fk], w=[fk])
                        A("scalar", R.activation(out=PS[7][:], in_=f[:], func=AF.Exp, scale=-0.5), r=[fk], w=["ps7"])
                        for i in range(4):
                            f, fk = ft()
                            A("vector", R.tensor_tensor(out=f[:], in0=U2[i][:], in1=PS[6][:], op=ALU.subtract),
                              r=["U2_%d" % i, "ps6"], w=[fk])
                            A("vector", R.tensor_tensor(out=f[:], in0=f[:], in1=PS[7][:], op=ALU.mult),
                              r=[fk, "ps7"], w=[fk])
                            A("scalar", R.activation(
                                out=Yj[:, 4 + i, :], in_=f[:], func=AF.Silu, scale=ppc(l, O_LG + i), bias=ppc(l, O_LB + i)),
                              r=[fk, "PP"], w=[yk])

                    def second(j, v=v, SRC=SRC, DST=DST, ntile=ntile, tiles=tiles):
                        a0 = tiles[j][0]
                        CUR_TW[0] = tiles[j][1]
                        residual_out(l, 2, v, WO, "WO", YB[j % 2], "Y%d" % (j % 2), KC, DST, PADL + a0 - 15)
                        if j + 1 < ntile:
                            load_x(SRC, PADL + tiles[j + 1][0] - 15, key="xr", dst=xr, tw=tiles[j + 1][1])
                    load_x(SRC, PADL - 15, key="xr", dst=xr, tw=tiles[0][1])
                    pre(0)
                    first(0)
                    if ntile > 1:
                        pre(1)
                    for j in range(ntile):
                        if j + 1 < ntile:
                            first(j + 1)
                        if j + 2 < ntile:
                            pre(j + 2)
                        second(j)
                    tix += ntile
                S.fence()
        else:
            o_ = l // 2
            with contextlib.ExitStack() as ph:
                uid[0] += 1
                sb = lambda name, shape, dt, u=uid[0]: ph.enter_context(nc.sbuf_tensor("%s_%d" % (name, u), shape, dt))
                WQ = sb("WQ", [128, KC, NQ], BF16)
                WO = sb("WO", [128, KC, D], BF16)
                QR = sb("QR", [128, KC, 8 * 128], BF16)
                KR = sb("KR", [128, 4, 8 * 128], BF16)
                VR = sb("VR", [128, 8, 4, 128], BF16)
                QC = sb("QC", [128, KC, NCTX], BF16)
                KCX = sb("KCX", [128, 4, NCTX], BF16)
                VCX = sb("VCX", [128, 4, 4, 128], BF16)
                ROPE = _W(sb("ROPE", [128, 2, TW], F32))
                EX = [sb("EX%d" % i, [128, TW], BF16) for i in range(4)]
                AOT = sb("AO", [128, KC, TW], BF16)
                AO = AOT
                AOW = _W(AOT)
                ESK = sb("ESK", [128, 16], F32)
                RD = [sb("RD%d" % i, [64, TW], F32) for i in range(2)]
                RT = [_W(sb("RT%d" % i, [128, TW], F32)) for i in range(4)]
                load_wc(WQ, "WQ", w_qkv[o_], KC, [(c, c, 256) for c in (list(range(2048, 3072, 256)) + list(range(0, 2048, 256)) + [3072])])
                load_wc(WO, "WO", w_o[o_], KC, [(c, c, 256) for c in range(0, D, 256)])
                A("gpsimd", R.memset(VR[:], 1.0), w=["VR%d" % s for s in range(8)])
                A("gpsimd", R.memset(VCX[:], 1.0), w=["VCX"])
                A("gpsimd", R.memset(AO[:], 0.0), w=["AO"])
                A("scalar", R.activation(out=ESK[:], in_=PP[:, l, O_SNK:O_SNK + 16], func=AF.Exp), r=["PP"], w=["ESK"], small=True)
                HORD = [0, 2, 1, 3]
                rt_rr = [0]
                ex_rr = [0]
                pv_rr = [0]
                pend_pv = [None]

                def attend(qsrc, qkey, qcol, kchunks, aocol):
                    CUR_TW[0] = TW
                    nkc = len(kchunks)
                    items = [(g, ci) for g in range(4) for ci in range(nkc)]
                    pvbank = {}
                    for g in range(4):
                        pvbank[g] = 6 + pv_rr[0]
                        pv_rr[0] ^= 1

                    def s_stage(g, ci):
                        (kget, kkey, vget, vkey, mi) = kchunks[ci]
                        ei = ex_rr[0]
                        ex_rr[0] = (ei + 1) % 4
                        for half in range(2):
                            b = ps_alloc()
                            A("tensor", R.matmul(
                                PS[b][:, 0:256],
                                lhsT=kget(g)[half * 64:(half + 1) * 64, :],
                                rhs=qsrc[half * 64:(half + 1) * 64, 2 * g:2 * g + 2, qcol:qcol + 128],
                                start=True, stop=True), r=[kkey, qkey], w=["ps%d" % b])
                            A("scalar", R.activation(out=EX[ei][:, half * 256:(half + 1) * 256], in_=PS[b][:, 0:256], func=AF.Exp, scale=0.125),
                              r=["ps%d" % b], w=["EX%d" % ei])
                        if mi is not None:
                            A("vector", R.tensor_tensor(out=EX[ei][:], in0=EX[ei][:], in1=MASK[:, mi, :], op=ALU.mult),
                              r=["EX%d" % ei, "MASK"], w=["EX%d" % ei])
                        return ei

                    def pv_stage(g, ci, ei):
                        (kget, kkey, vget, vkey, mi) = kchunks[ci]
                        pvb = pvbank[g]
                        A("tensor", R.matmul(
                            PS[pvb][:], lhsT=vget(g), rhs=EX[ei][:], start=(ci == 0), stop=(ci == nkc - 1)),
                          r=["EX%d" % ei, vkey], w=["ps%d" % pvb])
                        if ci != nkc - 1:
                            return
                        rd = RD[g % 2]
                        rk = "RD%d" % (g % 2)
                        for cb in range(4):
                            hd = g * 4 + HORD[cb]
                            A("scalar", R.activation(
                                out=rd[:, cb * 128:(cb + 1) * 128], in_=PS[pvb][64:128, cb * 128:(cb + 1) * 128],
                                func=AF.Ln, bias=ESK[64:128, hd:hd + 1], scale=1.0),
                              r=["ps%d" % pvb, "ESK"], w=[rk])
                        A("scalar", R.activation(out=rd[:], in_=rd[:], func=AF.Exp, scale=-1.0), r=[rk], w=[rk])
                        for half in range(2):
                            A("vector", R.tensor_tensor(
                                out=AO[half * 64:(half + 1) * 64, 2 * g:2 * g + 2, aocol:aocol + 128],
                                in0=PS[pvb][0:64, half * 256:(half + 1) * 256].rearrange("p (a t) -> p a t", a=2),
                                in1=rd[:, half * 256:(half + 1) * 256].rearrange("p (a t) -> p a t", a=2), op=ALU.mult),
                              r=["ps%d" % pvb, rk], w=["AO"])
                    for (g, ci) in items:
                        ei = s_stage(g, ci)
                        if pend_pv[0] is not None:
                            pend_pv[0]()
                        pend_pv[0] = (lambda g=g, ci=ci, ei=ei: pv_stage(g, ci, ei))

                tix = 0
                for (sname, v) in ([("ctx", 1), ("main", 0)]):
                    SRC, DST = (CB[0], CB[1]) if v else (XB[0], XB[1])
                    tiles = geom(v, False)
                    ntile = len(tiles)
                    tbase = tix

                    def pre(j, v=v, SRC=SRC, tbase=tbase, ntile=ntile, tiles=tiles):
                        CUR_TW[0] = tiles[j][1]
                        if j == 0:
                            load_x(SRC, PADL)
                        norm_mod(l, 1, 0, v, (tbase + j) % 2, nvalid=(NCTX if v else TW))
                        if j + 1 < ntile:
                            load_x(SRC, PADL + tiles[j + 1][0], tw=tiles[j + 1][1])
                    pre(0)
                    for j in range(ntile):
                        a0, tw = tiles[j]
                        CUR_TW[0] = tw
                        nblk = tw // 128
                        hs = tix % 2
                        tix += 1
                        h = hb[hs]
                        hk = "h%d" % hs
                        lag = 0 if v else 128
                        if not (v and last):
                            load_x(SRC, PADL + a0 - lag, key="xr", dst=xr)
                        rope = ROPE
                        rkey = "ROPE"
                        if (not v) and j == 0:
                            A("sync", R.dma_start(out=ROPE[:], in_=rope_in[:, :, a0:a0 + tw].rearrange("a p t -> p a t")),
                              w=["ROPE"], dma=True)
                        for m in (8, 9, 10, 11, 0, 1, 2, 3, 4, 5, 6, 7):
                            if m < 8:
                                c0, c1 = m * 128, 1024 + m * 128
                            else:
                                c0, c1 = 2048 + (m - 8) * 128, 2560 + (m - 8) * 128
                            if v and last and m < 8:
                                continue
                            if v:
                                bq = ps_alloc()
                                for k in range(KC):
                                    A("tensor", R.matmul(
                                        PS[bq][:], lhsT=WQ[:, k, c0:c0 + 128], rhs=h[:, k, :], start=(k == 0), stop=(k == KC - 1)),
                                      r=["WQ_%d" % (c0 // 128), hk], w=["ps%d" % bq])
                                dst_ap, dk_ = (QC[:, m, :], "QC") if m < 8 else (KCX[:, m - 8, :], "KCX")
                                A("scalar", R.copy(out=dst_ap, in_=PS[bq][:, 0:NCTX]), r=["ps%d" % bq], w=[dk_])
                                continue
                            bq = ps_alloc()
                            br = ps_alloc()
                            for (b, cc) in ((bq, c0), (br, c1)):
                                for k in range(KC):
                                    A("tensor", R.matmul(
                                        PS[b][:], lhsT=WQ[:, k, cc:cc + 128], rhs=h[:, k, :], start=(k == 0), stop=(k == KC - 1)),
                                      r=["WQ_%d" % (cc // 128), hk], w=["ps%d" % b])
                            i1 = rt_rr[0]
                            i2 = (i1 + 1) % 4
                            rt_rr[0] = (i1 + 2) % 4
                            A("vector", R.tensor_tensor(out=RT[i1][:], in0=PS[bq][:], in1=rope[:, 0, :], op=ALU.mult),
                              r=["ps%d" % bq, rkey], w=["RT%d" % i1])
                            A("vector", R.tensor_tensor(out=RT[i2][:], in0=PS[br][:], in1=rope[:, 1, :], op=ALU.mult),
                              r=["ps%d" % br, rkey], w=["RT%d" % i2])
                            if v:
                                if m < 8:
                                    dst_ap, dkeys = QC[:, m, :], ["QC"]
                                else:
                                    dst_ap, dkeys = KCX[:, m - 8, :], ["KCX"]
                            else:
                                sl0 = (4 * j) % 8
                                if m < 8:
                                    dst_ap, dkeys = QR[:, m, sl0 * 128:sl0 * 128 + tw], ["QR%d" % (sl0 + s) for s in range(nblk)]
                                else:
                                    dst_ap, dkeys = KR[:, m - 8, sl0 * 128:sl0 * 128 + tw], ["KR%d" % (sl0 + s) for s in range(nblk)]
                            A("gpsimd", R.tensor_tensor(out=dst_ap, in0=RT[i1][:], in1=RT[i2][:], op=ALU.add),
                              r=["RT%d" % i1, "RT%d" % i2], w=dkeys)
                        if (not v) and j + 1 < ntile:
                            n0, ntw = tiles[j + 1]
                            A("sync", R.dma_start(out=ROPE.t[:, :, 0:ntw], in_=rope_in[:, :, n0:n0 + ntw].rearrange("a p t -> p a t")),
                              w=["ROPE"], dma=True)
                        for blk in range(nblk):
                            b = ps_alloc()
                            for k in range(KC):
                                A("tensor", R.matmul(
                                    PS[b][:, 0:256], lhsT=h[:, k, blk * 128:(blk + 1) * 128], rhs=WQ[:, k, 3072:3328],
                                    start=(k == 0), stop=(k == KC - 1)), r=["WQ_24", "WQ_25", hk], w=["ps%d" % b])
                            if v:
                                dst_ap, dk = VCX[:, blk, :, 0:64], "VCX"
                            else:
                                sl = (4 * j + blk) % 8
                                dst_ap, dk = VR[:, sl, :, 0:64], "VR%d" % sl
                            A("scalar", R.copy(out=dst_ap, in_=PS[b][:, 0:256].rearrange("p (g d) -> p g d", g=4)),
                              r=["ps%d" % b], w=[dk])
                        if j + 1 < ntile:
                            pre(j + 1)
                        if v and last:
                            continue
                        ctxch = [((lambda g, cc=cc: KCX[:, g, cc * 128:(cc + 1) * 128]), "KCX",
                                  (lambda g, cc=cc: VCX[:, cc, g, :]), "VCX", None) for cc in range(2)]
                        if v:
                            for qb in range(2):
                                attend(QC, "QC", qb * 128, ctxch, qb * 128)
                        else:
                            for qi in range(nblk):
                                qb = 4 * j - 1 + qi
                                if qb < 0:
                                    continue
                                chs = []
                                for (kb, mi) in ((qb - 1, 0), (qb, None), (qb + 1, 1)):
                                    if kb < 0:
                                        continue
                                    sl = kb % 8
                                    chs.append(((lambda g, sl=sl: KR[:, g, sl * 128:(sl + 1) * 128]), "KR%d" % sl,
                                                (lambda g, sl=sl: VR[:, sl, g, :]), "VR%d" % sl, mi))
                                sq_ = qb % 8
                                attend(QR, "QR%d" % sq_, sq_ * 128, chs + ctxch, qi * 128)
                        if pend_pv[0] is not None:
                            pend_pv[0]()
                            pend_pv[0] = None
                        CUR_TW[0] = tw
                        residual_out(l, 2, v, WO, "WO", AOW, "AO", KC, DST, PADL + a0 - lag)
                S.fence()
        if stop_after == ("m", l):
            break
        for hf in range(2):
            with contextlib.ExitStack() as ph:
                uid[0] += 1
                sb = lambda name, shape, dt, u=uid[0]: ph.enter_context(nc.sbuf_tensor("%s_%d" % (name, u), shape, dt))
                WU = sb("WU", [128, KC, 2816], BF16)
                WD = sb("WD", [128, 11, D], BF16)
                DG = sb("DG", [128, 66, 128], BF16)
                EXF = [sb("EXF%d" % i, [128, 2 + TW], BF16) for i in range(4)]
                CAR = sb("CAR", [128, 22, 2], BF16)
                SG = [_W(sb("SG%d" % i, [128, TW], F32)) for i in range(2)]
                ACT = _W(sb("ACT", [128, 11, TW], BF16))
                ACT2 = _W(sb("ACT2", [128, 11, TW], BF16))
                ublk = []
                for c in range(0, 1408, 256):
                    n = min(256, 1408 - c)
                    ublk.append((c, hf * 1408 + c, n))
                    ublk.append((1408 + c, 2816 + hf * 1408 + c, n))
                load_wc(WU, "WU", w_up[l], KC, ublk)
                load_wc(WD, "WD", w_dn[l, hf * 1408:(hf + 1) * 1408, :], 11, [(c, c, 256) for c in range(0, D, 256)])
                for ci in range(22):
                    gch = (hf * 11 + ci) if ci < 11 else (22 + hf * 11 + ci - 11)
                    for k in range(3):
                        A("gpsimd", R.tensor_scalar(
                            out=DG[:, ci * 3 + k, :], in0=IDN[:], scalar1=ppc(l, O_WCF + gch * 3 + k), scalar2=1.0, op0=ALU.mult, op1=ALU.mult),
                          r=["IDN", "PP"], w=["DG"])
                ACTB = [ACT, ACT2]
                tix = 0
                ex_rr = [0]
                pp_rr = [0]
                cp_rr = [0]

                def palloc():
                    b = 1 + pp_rr[0]
                    pp_rr[0] = (pp_rr[0] + 1) % 3
                    return b

                def calloc():
                    b = 4 + cp_rr[0]
                    cp_rr[0] = (cp_rr[0] + 1) % 4
                    return b
                DEPTHP = 2
                for (sname, v) in streams(l, last):
                    Bn, Br, Bd = (CB if v else XB)[1], (CB if v else XB)[1 + hf], (CB if v else XB)[(2 + hf) % 3]
                    tiles = geom(v, True)
                    ntile = len(tiles)
                    A("gpsimd", R.memset(CAR[:], 0.0), w=["CAR"])
                    tbase = tix

                    def pre(j, v=v, Bn=Bn, tbase=tbase, ntile=ntile, tiles=tiles):
                        CUR_TW[0] = tiles[j][1]
                        if j == 0:
                            load_x(Bn, PADL)
                        norm_mod(l, 4, 3, v, (tbase + j) % 2, nvalid=(NCTX if v else TW))
                        if j + 1 < ntile:
                            load_x(Bn, PADL + tiles[j + 1][0], tw=tiles[j + 1][1])

                    def first(j, tbase=tbase, tiles=tiles):
                        tw = tiles[j][1]
                        CUR_TW[0] = tw
                        hs = (tbase + j) % 2
                        h = hb[hs]
                        hk = "h%d" % hs
                        actb = ACTB[j % 2]
                        ak = "ACT%d" % (j % 2)
                        pend = []
                        convb = {}

                        def tail(item):
                            ci, nm, pj, ei = item
                            ek = "EXF%d" % ei
                            bc = calloc()
                            conv_pe(bc, DG, ci, 3, EXF[ei], ek, 0, "DG")
                            convb[(nm, pj)] = bc
                            if nm == "g":
                                sg = SG[pj % 2]
                                sk = "SG%d" % (pj % 2)
                                bg, ba = convb.pop(("g", pj)), convb.pop(("a", pj))
                                A("scalar", R.activation(out=sg[:], in_=PS[bg][:], func=AF.Silu), r=["ps%d" % bg], w=[sk])
                                A("vector", R.tensor_tensor(out=actb[:, pj, :], in0=PS[ba][:], in1=sg[:], op=ALU.mult),
                                  r=["ps%d" % ba, sk], w=[ak])
                        for pj in range(11):
                            for (ci, nm) in ((pj, "a"), (11 + pj, "g")):
                                b = palloc()
                                c0 = ci * 128
                                for k in range(KC):
                                    A("tensor", R.matmul(
                                        PS[b][:], lhsT=WU[:, k, c0:c0 + 128], rhs=h[:, k, :], start=(k == 0), stop=(k == KC - 1)),
                                      r=["WU_%d" % ci, hk], w=["ps%d" % b])
                                ei = ex_rr[0]
                                ex_rr[0] = (ei + 1) % 4
                                ek = "EXF%d" % ei
                                A("gpsimd", R.tensor_copy(out=EXF[ei][:, 0:2], in_=CAR[:, ci, :]), r=["CAR"], w=[ek])
                                A("scalar", R.copy(out=EXF[ei][:, 2:2 + tw], in_=PS[b][:]), r=["ps%d" % b], w=[ek])
                                A("gpsimd", R.tensor_copy(out=CAR[:, ci, :], in_=EXF[ei][:, tw:tw + 2]), r=[ek], w=["CAR"])
                                pend.append((ci, nm, pj, ei))
                                if len(pend) > DEPTHP:
                                    tail(pend.pop(0))
                        while pend:
                            tail(pend.pop(0))

                    def second(j, v=v, Br=Br, Bd=Bd, ntile=ntile, tiles=tiles):
                        a0 = tiles[j][0]
                        CUR_TW[0] = tiles[j][1]
                        residual_out(l, 5, v, WD, "WD", ACTB[j % 2], "ACT%d" % (j % 2), 11, Bd, PADL + a0 - 1, alloc=palloc)
                        if j + 1 < ntile:
                            load_x(Br, PADL + tiles[j + 1][0] - 1, key="xr", dst=xr, tw=tiles[j + 1][1])
                    load_x(Br, PADL - 1, key="xr", dst=xr, tw=tiles[0][1])
                    pre(0)
                    first(0)
                    if ntile > 1:
                        pre(1)
                    for j in range(ntile):
                        if j + 1 < ntile:
                            first(j + 1)
                        if j + 2 < ntile:
                            pre(j + 2)
                        second(j)
                    tix += ntile
                S.fence()
        if stop_after == ("f", l):
            break

    finals = []
    CUR_TW[0] = TW
    if stop_after is None:
        for j in range(HALF // TW):
            a0 = j * TW
            load_x(XB[0], PADL + a0)
            norm_mod(0, 0, 0, 0, 0, gain_ap=lambda c: FN[:, c:c + 1], out_f32=xr)
            for c in range(KC):
                finals.append(A("sync", R.dma_start(out=out[c * 128:(c + 1) * 128, a0:a0 + TW], in_=xr[:, c, :]),
                                r=["xr%d" % c], w=["dram_out"], dma=True))
    else:
        finals = [d for d in S.dma_last if d is not None]
    S.emit(final_wait_ops=finals)
    outer.close()
    return nc


def _fm(vec):
    v = np.asarray(vec, np.float32).reshape(-1, 8, 128)
    return np.ascontiguousarray(v.transpose(2, 0, 1).reshape(128, -1))


def _prep_common(inp):
    w_qkv = np.asarray(inp["w_qkv"], np.float32)
    part = np.concatenate([np.arange(32, 64), np.arange(0, 32)])
    qcols = np.arange(1024)
    qrot = (np.arange(16)[:, None] * 64 + part[None, :]).reshape(-1)
    kd, kdr = [], []
    for g in range(4):
        base = 1024 + g * 64
        kd += [base + np.arange(64), base + np.arange(64)]
        kdr += [base + part, base + part]
    cols = np.concatenate([qcols, qrot, np.concatenate(kd), np.concatenate(kdr), 1280 + np.arange(256)])
    w_qkv_ext = np.ascontiguousarray(w_qkv[:, :, cols])
    tri = np.tril(np.ones((128, 128), np.float32))
    maskP = np.tile(tri, (1, 4))
    maskN = np.tile(tri.T, (1, 4))
    masks = np.ascontiguousarray(np.concatenate([maskP, maskN], axis=1))
    return dict(
        w_mod=np.ascontiguousarray(inp["w_mod"], np.float32), w_in_ab=np.ascontiguousarray(inp["w_in_ab"], np.float32),
        w_out_ab=np.ascontiguousarray(inp["w_out_ab"], np.float32), w_qkv_ext=w_qkv_ext,
        w_o=np.ascontiguousarray(inp["w_o"], np.float32), w_up=np.ascontiguousarray(inp["w_up"], np.float32),
        w_down=np.ascontiguousarray(inp["w_down"], np.float32), ident=np.eye(128, dtype=np.float32), masks=masks,
        fnorm=_fm(inp["final_norm"]))


def _prep_core(inp, core):
    b, half = core // 2, core % 2
    x = np.asarray(inp["x"], np.float32)[b]
    ctx = np.asarray(inp["ctx"], np.float32)[b]
    if half:
        x = x[::-1]
        ctx = ctx[::-1]
    x_fm = np.zeros((D, XW), np.float32)
    x_fm[:, PADL:] = x[:T].T
    c_fm = np.zeros((D, CW), np.float32)
    c_fm[:, PADL:PADL + NCTX] = ctx.T
    cv = np.stack([np.asarray(inp["c"], np.float32)[b], np.asarray(inp["c_ctx"], np.float32)], 0)
    cvec = np.ascontiguousarray(cv.reshape(2, 8, 128).transpose(2, 1, 0).reshape(128, 16))
    flip = (lambda a: a[::-1]) if half else (lambda a: a)
    pp = np.zeros((128, DEPTH, NPP), np.float32)
    for l in range(DEPTH):
        pp[:, l, O_NM:O_NM + 8] = _fm(inp["norm_mix"][l])
        pp[:, l, O_NF:O_NF + 8] = _fm(inp["norm_ffn"][l])
        pp[:, l, O_BM:O_BM + 48] = _fm(inp["b_mod"][l])
        if l % 2 == 0:
            e = l // 2
            ca = flip(np.asarray(inp["conv_a"], np.float32)[e])
            pp[:, l, O_CA:O_CA + 12] = ca.reshape(3, 4, 128).transpose(2, 1, 0).reshape(128, 12)
            cbw = flip(np.asarray(inp["conv_b"], np.float32)[e])
            pp[:, l, O_CB:O_CB + 124] = cbw.reshape(31, 4, 128).transpose(2, 1, 0).reshape(128, 124)
            pp[:, l, O_CBB:O_CBB + 4] = np.asarray(inp["conv_b_bias"], np.float32)[e].reshape(4, 128).T
            pp[:, l, O_LG:O_LG + 4] = np.asarray(inp["ln_b_gain"], np.float32)[e].reshape(4, 128).T
            pp[:, l, O_LB:O_LB + 4] = np.asarray(inp["ln_b_bias"], np.float32)[e].reshape(4, 128).T
        else:
            pp[:, l, O_SNK:O_SNK + 16] = np.asarray(inp["sinks"], np.float32)[l // 2][None, :]
        wc = flip(np.asarray(inp["w_conv_ffn"], np.float32)[l])
        pp[:, l, O_WCF:O_WCF + 132] = wc.reshape(3, 44, 128).transpose(2, 1, 0).reshape(128, 132)
    j = np.arange(T)
    pos = (SEQ - 1 - j) if half else j
    row = (pos // 64).astype(np.float32)
    col = (pos % 64).astype(np.float32)
    inv = (np.float32(10000.0) ** (-np.arange(16, dtype=np.float32) / np.float32(16))).astype(np.float32)
    ang = np.concatenate([row[:, None] * inv[None, :], col[:, None] * inv[None, :]], axis=1).astype(np.float32)
    cs, sn = np.cos(ang).astype(np.float32), np.sin(ang).astype(np.float32)
    cos64 = np.concatenate([cs, cs], axis=1).T
    sin64 = np.concatenate([-sn, sn], axis=1).T
    rope = np.stack([np.concatenate([cos64, cos64], 0), np.concatenate([sin64, sin64], 0)], 0)
    return dict(x_fm=x_fm, ctx_fm=c_fm, cvec=cvec, pp=np.ascontiguousarray(pp.reshape(128, DEPTH * NPP)),
                rope=np.ascontiguousarray(rope, np.float32))


_NC_CACHE = {}


def kernel(**inputs):
    common = _prep_common(inputs)
    in_maps = []
    for core in range(8):
        m = dict(common)
        m.update(_prep_core(inputs, core))
        in_maps.append(m)
    if "nc" not in _NC_CACHE:
        _NC_CACHE["nc"] = build_program()
    res = run_bass_kernel_spmd(_NC_CACHE["nc"], in_maps, core_ids=list(range(8)))
    outp = np.empty((4, SEQ, D), np.float32)
    for core in range(8):
        b, half = core // 2, core % 2
        o = res.results[core]["out_fm"].T
        if half:
            outp[b, HALF:] = o[::-1]
        else:
            outp[b, :HALF] = o
    return outp
```

```python
import contextlib
import numpy as np
import concourse.bass as bass
import concourse.mybir as mybir
from concourse.bass_utils import run_bass_kernel_spmd

F32 = mybir.dt.float32
BF16 = mybir.dt.bfloat16
AF = mybir.ActivationFunctionType
ALU = mybir.AluOpType

ENGS = ("tensor", "vector", "scalar", "gpsimd", "sync")

D = 1024
KC = 8
TW = 512
NT = 9
T = 4480
WIDTHS = [512] * 8 + [384]
CTXW_CONV = 288
PADL = 128
XW = PADL + T
CW = PADL + TW
SEQ = 8192
HALF = 4096
NCTX = 256
DEPTH = 4
EPS = 1e-6
O_NM, O_NF, O_BM, O_CA, O_CB, O_CBB, O_LG, O_LB, O_WCF, O_SNK, NPP = 0, 8, 16, 64, 76, 200, 204, 208, 212, 344, 360
NQ = 3328


class _Op:
    __slots__ = ("eng", "fn", "deps", "is_dma", "sem", "val", "need_sig", "sig_idx", "small")

    def __init__(self, eng, fn, is_dma):
        self.eng = eng
        self.fn = fn
        self.deps = []
        self.is_dma = is_dma
        self.sem = None
        self.val = 0
        self.need_sig = False
        self.sig_idx = 0
        self.small = False


CUR_TW = [TW]


class _W:
    def __init__(self, t):
        self.t = t

    def __getitem__(self, idx):
        if not isinstance(idx, tuple):
            idx = (idx,)
        nd = len(self.t.shape)
        if len(idx) == 1 and nd > 1 and idx[0] == slice(None):
            idx = (slice(None),) * nd
        if len(idx) == nd and idx[-1] == slice(None):
            idx = idx[:-1] + (slice(0, CUR_TW[0]),)
        return self.t[idx]


class _Rec:
    def __getattr__(self, name):
        def f(*args, **kwargs):
            return (name, args, kwargs)
        return f


R = _Rec()


class Sched:
    def __init__(self, nc, n_dma_sems=24, same_engine_sync=True):
        self.nc = nc
        self.ops = {e: [] for e in ENGS}
        self.last_w = {}
        self.readers = {}
        self.n_dma_sems = n_dma_sems
        self.dma_rr = 0
        self.dma_last = [None] * n_dma_sems
        self.dma_cnt = [0] * n_dma_sems
        self.same_engine_sync = same_engine_sync
        self.pending_fence = {e: [] for e in ENGS}

    def _dep(self, op, d):
        if d is None or d is op:
            return
        if (not d.is_dma) and (not op.is_dma) and d.eng == op.eng:
            if d.eng == "tensor" or not self.same_engine_sync:
                return
            if d.eng in ("scalar", "vector") and not d.small and not op.small:
                return
        op.deps.append(d)
        if not d.is_dma:
            d.need_sig = True

    def fence(self):
        lasts = []
        for e in ENGS:
            comp = [o for o in self.ops[e] if not o.is_dma]
            if comp:
                lasts.append(comp[-1])
        for d in self.dma_last:
            if d is not None:
                lasts.append(d)
        for e in ENGS:
            self.pending_fence[e] = list(lasts)
        self.last_w = {}
        self.readers = {}

    def add(self, eng, fn, reads=(), writes=(), dma=False, small=False):
        op = _Op(eng, fn, dma)
        op.small = small
        if self.pending_fence[eng]:
            for d in self.pending_fence[eng]:
                if d.is_dma or d.eng != eng or dma:
                    op.deps.append(d)
                    if not d.is_dma:
                        d.need_sig = True
            self.pending_fence[eng] = []
        for k in reads:
            self._dep(op, self.last_w.get(k))
        for k in writes:
            self._dep(op, self.last_w.get(k))
            for r in self.readers.get(k, ()):
                if (not dma) and (not r.is_dma) and r.eng == eng:
                    continue
                self._dep(op, r)
        if dma:
            s = self.dma_rr
            self.dma_rr = (self.dma_rr + 1) % self.n_dma_sems
            prev = self.dma_last[s]
            if prev is not None:
                op.deps.append(prev)
            self.dma_cnt[s] += 16
            op.sem = s
            op.val = self.dma_cnt[s]
            self.dma_last[s] = op
        for k in reads:
            self.readers.setdefault(k, []).append(op)
        for k in writes:
            self.last_w[k] = op
            self.readers[k] = []
        self.ops[eng].append(op)
        return op

    def emit(self, final_wait_ops=()):
        nc = self.nc
        for e in ENGS:
            c = 0
            for op in self.ops[e]:
                if not op.is_dma and op.need_sig:
                    c += 1
                    op.sig_idx = c
        with contextlib.ExitStack() as st:
            esem = {e: st.enter_context(nc.semaphore("S_" + e)) for e in ENGS}
            dsem = [st.enter_context(nc.semaphore("D_%d" % i)) for i in range(self.n_dma_sems)]
            block = st.enter_context(nc.Block())

            def make(e):
                def body(eng):
                    waited = {}
                    for op in self.ops[e]:
                        for d in op.deps:
                            if d.is_dma:
                                key, sem, val = ("d", d.sem), dsem[d.sem], d.val
                            else:
                                key, sem, val = ("e", d.eng), esem[d.eng], d.sig_idx
                            if waited.get(key, 0) >= val:
                                continue
                            waited[key] = val
                            eng.wait_ge(sem, val)
                        ins = getattr(eng, op.fn[0])(*op.fn[1], **op.fn[2])
                        if op.is_dma:
                            ins.then_inc(dsem[op.sem], 16)
                        elif op.need_sig:
                            ins.then_inc(esem[e], 1)
                    if e == "sync":
                        for d in final_wait_ops:
                            eng.wait_ge(dsem[d.sem], d.val)
                return body

            block.tensor(make("tensor"))
            block.vector(make("vector"))
            block.scalar(make("scalar"))
            block.gpsimd(make("gpsimd"))
            block.sync(make("sync"))


def build_program(n_layers=DEPTH, nt=NT, dbg=False, stop_after=None, widths=None):
    nc = bass.Bass("TRN2", target_bir_lowering=False)
    dt_in = lambda name, shape: nc.dram_tensor(name, shape, F32, kind="ExternalInput").ap()
    x_in = dt_in("x_fm", [D, XW])
    c_in = dt_in("ctx_fm", [D, CW])
    cvec = dt_in("cvec", [128, 16])
    pp_in = dt_in("pp", [128, DEPTH * NPP])
    fn_in = dt_in("fnorm", [128, 8])
    ident_in = dt_in("ident", [128, 128])
    masks_in = dt_in("masks", [128, 1024])
    rope_in = dt_in("rope", [2, 128, T])
    w_mod = dt_in("w_mod", [DEPTH, D, 6 * D])
    w_in = dt_in("w_in_ab", [2, D, 2560])
    w_out = dt_in("w_out_ab", [2, D, D])
    w_qkv = dt_in("w_qkv_ext", [2, D, NQ])
    w_o = dt_in("w_o", [2, D, D])
    w_up = dt_in("w_up", [DEPTH, D, 5632])
    w_dn = dt_in("w_down", [DEPTH, 2816, D])
    out = nc.dram_tensor("out_fm", [D, HALF], F32, kind="ExternalOutput").ap()
    skind = "ExternalOutput" if dbg else "Internal"
    XB = [nc.dram_tensor("xs%d" % i, [D, XW], F32, kind=skind).ap() for i in range(3)]
    CB = [nc.dram_tensor("cs%d" % i, [D, CW], F32, kind=skind).ap() for i in range(3)]
    dkey = {}
    for i in range(3):
        dkey[id(XB[i])] = "dX%d" % i
        dkey[id(CB[i])] = "dC%d" % i
    dkey[id(x_in)] = "dXin"
    dkey[id(c_in)] = "dCin"

    S = Sched(nc)
    outer = contextlib.ExitStack()
    sbo = lambda name, shape, dt: outer.enter_context(nc.sbuf_tensor(name, shape, dt))
    PS = [_W(outer.enter_context(nc.psum_tensor("psb%d" % i, [128, TW], F32))) for i in range(8)]
    widths = list(widths if widths is not None else WIDTHS[:nt])
    starts = [sum(widths[:i]) for i in range(len(widths))]
    pool_rr = [0]

    def ps_alloc():
        b = 1 + pool_rr[0]
        pool_rr[0] = (pool_rr[0] + 1) % 5
        return b

    PP = sbo("PP", [128, DEPTH, NPP], F32)
    FN = sbo("FN", [128, 8], F32)
    MV = sbo("MV", [128, DEPTH * 6 * 8 * 2], F32)
    IDN = sbo("IDN", [128, 128], F32)
    ONESB = sbo("ONESB", [128, 128], BF16)
    MASK = sbo("MASK", [128, 2, TW], BF16)
    xt = _W(sbo("xt", [128, KC, TW], F32))
    xr = _W(sbo("xr", [128, KC, TW], F32))
    hb = [_W(sbo("h%d" % i, [128, KC, TW], BF16)) for i in range(2)]
    sq = [_W(sbo("sq%d" % i, [128, TW], BF16)) for i in range(2)]
    stdt = _W(sbo("stdt", [128, TW], F32))
    htmp = [_W(sbo("htmp%d" % i, [128, TW], F32)) for i in range(2)]

    def geom(v, conv):
        if v:
            return [(0, CTXW_CONV if conv else NCTX)]
        return list(zip(starts, widths))

    def mv(l, j, c, v):
        o = ((l * 6 + j) * 8 + c) * 2 + v
        return MV[:, o:o + 1]

    def ppc(l, col):
        return PP[:, l, col:col + 1]

    small_mode = [False]

    def A(eng, fn, r=(), w=(), dma=False, small=False):
        return S.add(eng, fn, reads=r, writes=w, dma=dma, small=(small or small_mode[0]))

    A("sync", R.dma_start(out=PP[:], in_=pp_in.rearrange("p (l n) -> p l n", l=DEPTH)), w=["PP"], dma=True)
    A("sync", R.dma_start(out=FN[:], in_=fn_in), w=["FN"], dma=True)
    A("sync", R.dma_start(out=IDN[:], in_=ident_in), w=["IDN"], dma=True)
    A("gpsimd", R.dma_start(out=MASK[:], in_=masks_in.rearrange("p (a t) -> p a t", a=2)), w=["MASK"], dma=True)
    A("vector", R.memset(ONESB[:], 1.0), w=["ONESB"])

    with contextlib.ExitStack() as ph:
        small_mode[0] = True
        sb = lambda name, shape, dt: ph.enter_context(nc.sbuf_tensor(name, shape, dt))
        ZT = sb("ZT", [128, CW], F32)
        A("vector", R.memset(ZT[:], 0.0), w=["ZT"])
        for i in range(3):
            for c in range(KC):
                A("sync", R.dma_start(out=XB[i][c * 128:(c + 1) * 128, 0:PADL], in_=ZT[:, 0:PADL]),
                  r=["ZT"], dma=True)
                A("sync", R.dma_start(out=XB[i][c * 128:(c + 1) * 128, XW - 256:XW], in_=ZT[:, 0:256]),
                  r=["ZT"], dma=True)
                A("sync", R.dma_start(out=CB[i][c * 128:(c + 1) * 128, :], in_=ZT[:, :]),
                  r=["ZT"], dma=True)
        CV = sb("CV", [128, 16], F32)
        SCV = sb("SCV", [128, 16], F32)
        WM = [sb("WM%d" % i, [128, KC, 768], BF16) for i in range(4)]
        SCVB = sb("SCVB", [128, 16], BF16)
        MRAW = sb("MRAW", [128, 96], F32)
        A("sync", R.dma_start(out=CV[:], in_=cvec), w=["CV"], dma=True)
        A("scalar", R.activation(out=SCV[:], in_=CV[:], func=AF.Silu), r=["CV"], w=["SCV"])
        A("vector", R.tensor_copy(out=SCVB[:], in_=SCV[:]), r=["SCV"], w=["SCVB"])
        piece = 0
        for l in range(n_layers):
            for fp in range(8):
                wm = WM[piece % 4]
                wk = "WM%d" % (piece % 4)
                piece += 1
                for half in range(2):
                    A("gpsimd", R.dma_start(
                        out=wm[:, half * 4:(half + 1) * 4, :],
                        in_=w_mod[l, half * 512:(half + 1) * 512, fp * 768:(fp + 1) * 768].rearrange("(k p) n -> p k n", p=128),
                        max_dma_last_dim=4096),
                      w=["%s_%d" % (wk, half)], dma=True)
                for m in range(6):
                    ch = fp * 6 + m
                    for kc in range(KC):
                        A("tensor", R.matmul(
                            PS[0][:, ch * 2:ch * 2 + 2], lhsT=wm[:, kc, m * 128:(m + 1) * 128],
                            rhs=SCVB[:, kc * 2:kc * 2 + 2], start=(kc == 0), stop=(kc == KC - 1)),
                          r=["%s_%d" % (wk, kc // 4), "SCVB"], w=["ps0"])
            for v in range(2):
                A("vector", R.tensor_tensor(
                    out=MRAW[:, v:96:2], in0=PS[0][:, v:96:2], in1=PP[:, l, O_BM:O_BM + 48], op=ALU.add),
                  r=["ps0", "PP"], w=["MRAW"])
            for v in range(2):
                def mraw(j, v=v):
                    return MRAW[:, (j * 8) * 2 + v:(j * 8 + 8) * 2:2]

                def mvs(j, l=l, v=v):
                    o = ((l * 6 + j) * 8) * 2 + v
                    return MV[:, o:o + 15:2]
                for j in (0, 2, 3, 5):
                    A("vector", R.tensor_copy(out=mvs(j), in_=mraw(j)),
                      r=["MRAW"], w=["MV"])
                for (j, ocol) in ((1, O_NM), (4, O_NF)):
                    A("vector", R.scalar_tensor_tensor(
                        out=mvs(j), in0=mraw(j), scalar=1.0, in1=PP[:, l, ocol:ocol + 8], op0=ALU.add, op1=ALU.mult),
                      r=["MRAW", "PP"], w=["MV"])
        S.fence()
        small_mode[0] = False

    def load_x(src, col0, key="xt", dst=None, tw=None):
        dst = xt if dst is None else dst
        tw = CUR_TW[0] if tw is None else tw
        for c in range(KC):
            A("sync", R.dma_start(out=dst[:, c, 0:tw], in_=src[c * 128:(c + 1) * 128, col0:col0 + tw]),
              r=[dkey[id(src)]], w=["%s%d" % (key, c)], dma=True)

    def norm_mod(l, jg, jsh, v, hslot, nvalid=TW, gain_ap=None, out_f32=None):
        h = hb[hslot]
        hk = "h%d" % hslot
        nvalid = min(nvalid, CUR_TW[0])
        for c in range(KC):
            s = sq[c % 2]
            sk = "sq%d" % (c % 2)
            A("scalar", R.activation(out=s[:], in_=xt[:, c, :], func=AF.Square), r=["xt%d" % c], w=[sk])
            A("tensor", R.matmul(PS[0][:], lhsT=ONESB[:], rhs=s[:], start=(c == 0), stop=(c == KC - 1)),
              r=[sk, "ONESB"], w=["ps0"])
        A("scalar", R.activation(out=stdt[:], in_=PS[0][:], func=AF.Ln, scale=1.0 / D, bias=EPSB[:, 0:1]),
          r=["ps0"], w=["stdt"])
        A("scalar", R.activation(out=PS[0][:], in_=stdt[:], func=AF.Exp, scale=-0.5), r=["stdt"], w=["ps0"])
        for c in range(KC):
            g = gain_ap(c) if gain_ap is not None else mv(l, jg, c, v)
            if out_f32 is not None:
                A("vector", R.scalar_tensor_tensor(
                    out=out_f32[:, c, :], in0=xt[:, c, :], scalar=g, in1=PS[0][:], op0=ALU.mult, op1=ALU.mult),
                  r=["xt%d" % c, "ps0", "MV", "FN"], w=["xr%d" % c])
                continue
            tmp = htmp[c % 2]
            tk = "htmp%d" % (c % 2)
            A("vector", R.scalar_tensor_tensor(
                out=tmp[:], in0=xt[:, c, :], scalar=g, in1=PS[0][:], op0=ALU.mult, op1=ALU.mult),
              r=["xt%d" % c, "ps0", "MV"], w=[tk])
            A("scalar", R.activation(
                out=h[:, c, 0:nvalid], in_=tmp[:, 0:nvalid], func=AF.Identity, bias=mv(l, jsh, c, v), scale=1.0),
              r=[tk, "MV"], w=[hk])
        if nvalid < CUR_TW[0]:
            A("gpsimd", R.memset(h[:, :, nvalid:CUR_TW[0]], 0.0), w=[hk])

    def load_w(dst, key, src_rows, ncols_list):
        for (k, c0, ap) in src_rows:
            n = ap.shape[1]
            A("gpsimd", R.dma_start(out=dst[:, k, c0:c0 + n], in_=ap, max_dma_last_dim=4096),
              r=["wdram"], w=["%s%d" % (key, k)], dma=True)

    def load_wc(dst, key, src2d, nk, blocks):
        for (d0, s0, n) in blocks:
            keys = ["%s_%d" % (key, c) for c in range(d0 // 128, (d0 + n + 127) // 128)]
            A("gpsimd", R.dma_start(out=dst[:, 0:nk, d0:d0 + n],
                                    in_=src2d[:, s0:s0 + n].rearrange("(k p) n -> p k n", p=128), max_dma_last_dim=4096),
              r=["wdram"], w=keys, dma=True)

    def build_diag(dst, l, col0, ntap, nchunk, key):
        for j in range(nchunk * ntap):
            A("vector", R.tensor_scalar(
                out=dst[:, j, :], in0=IDN[:], scalar1=ppc(l, col0 + j), scalar2=None, op0=ALU.mult),
              r=["IDN", "PP"], w=["%s_%d" % (key, j // ntap)])

    def conv_pe(bank, diag, j, ntap, ext, ekey, off0, dkey):
        for k in range(ntap):
            A("tensor", R.matmul(PS[bank][:], lhsT=diag[:, j * ntap + k, :], rhs=ext[:, off0 + k:off0 + k + CUR_TW[0]],
                                                 start=(k == 0), stop=(k == ntap - 1)),
              r=[ekey, "%s_%d" % (dkey, j)], w=["ps%d" % bank])

    def residual_out(l, jgate, v, W, wkey, rhs, rkey, nk, dst, col0, alloc=None):
        for oc in range(KC):
            b = (alloc or ps_alloc)()
            for k in range(nk):
                A("tensor", R.matmul(PS[b][:], lhsT=W[:, k, oc * 128:(oc + 1) * 128], rhs=rhs[:, k, :],
                                                              start=(k == 0), stop=(k == nk - 1)),
                  r=["%s_%d" % (wkey, oc), rkey], w=["ps%d" % b])
            A("vector", R.scalar_tensor_tensor(
                out=xr[:, oc, :], in0=PS[b][:], scalar=mv(l, jgate, oc, v), in1=xr[:, oc, :], op0=ALU.mult, op1=ALU.add),
              r=["ps%d" % b, "xr%d" % oc, "MV"], w=["xr%d" % oc])
            A("sync", R.dma_start(out=dst[oc * 128:(oc + 1) * 128, col0:col0 + CUR_TW[0]], in_=xr[:, oc, :]),
              r=["xr%d" % oc], w=[dkey[id(dst)]], dma=True)

    EPSB = sbo("EPSB", [128, 1], F32)
    A("vector", R.memset(EPSB[:], EPS), w=["EPSB"])

    uid = [0]
    streams = lambda l, last: ([("ctx", 1)] if not last else []) + [("main", 0)]

    for l in range(n_layers):
        last = (l == DEPTH - 1)
        even = (l % 2 == 0)
        ia, ib, ic = [(0, 1, 2), (0, 1, 2), (0, 1, 2), (0, 1, 2)][l]
        if even:
            e_ = l // 2
            with contextlib.ExitStack() as ph:
                uid[0] += 1
                sb = lambda name, shape, dt, u=uid[0]: ph.enter_context(nc.sbuf_tensor("%s_%d" % (name, u), shape, dt))
                WI = sb("WI", [128, KC, 2560], BF16)
                WO = sb("WO", [128, KC, D], BF16)
                DGA = sb("DGA", [128, 12, 128], BF16)
                DGB = sb("DGB", [128, 124, 128], BF16)
                EXT = [sb("EXT%d" % i, [128, 30 + TW], BF16) for i in range(4)]
                EXG = [sb("EXG%d" % i, [128, 30 + TW], F32) for i in range(4)]
                EXU = [sb("EXU%d" % i, [128, 30 + TW], BF16) for i in range(4)]
                U2 = [_W(sb("U2_%d" % i, [128, TW], F32)) for i in range(4)]
                U2B = [_W(sb("U2B%d" % i, [128, TW], BF16)) for i in range(2)]
                SQB = [_W(sb("SQB%d" % i, [128, TW], BF16)) for i in range(2)]
                FT = [_W(sb("FT%d" % i, [128, TW], F32)) for i in range(3)]
                YB = [_W(sb("Y0", [128, KC, TW], BF16)), _W(sb("Y1", [128, KC, TW], BF16))]
                load_wc(WI, "WI", w_in[e_], KC, [(c * 128, c * 128, 256) for c in (8, 4, 0, 10, 6, 2, 16, 12, 18, 14)])
                load_wc(WO, "WO", w_out[e_], KC, [(c, c, 256) for c in range(0, D, 256)])
                build_diag(DGA, l, O_CA, 3, 4, "DGA")
                build_diag(DGB, l, O_CB, 31, 4, "DGB")
                ft_rr = [0]

                def ft():
                    i = ft_rr[0]
                    ft_rr[0] = (i + 1) % 3
                    return FT[i], "FT%d" % i
                tix = 0
                for (sname, v) in streams(l, last):
                    SRC = ((c_in if l == 0 else CB[0]) if v else (x_in if l == 0 else XB[0]))
                    DST = CB[1] if v else XB[1]
                    tiles = geom(v, True)
                    ntile = len(tiles)
                    tbase = tix

                    def pre(j, v=v, SRC=SRC, tbase=tbase, ntile=ntile, tiles=tiles):
                        CUR_TW[0] = tiles[j][1]
                        if j == 0:
                            load_x(SRC, PADL)
                        norm_mod(l, 1, 0, v, (tbase + j) % 2, nvalid=(NCTX if v else TW))
                        if j + 1 < ntile:
                            load_x(SRC, PADL + tiles[j + 1][0], tw=tiles[j + 1][1])
                    for i in range(4):
                        A("gpsimd", R.memset(EXT[i][:, 0:30], 0.0), w=["EXT%d" % i])
                        A("gpsimd", R.memset(EXG[i][:, 0:30], 0.0), w=["EXG%d" % i])
                        A("gpsimd", R.memset(EXU[i][:, 0:30], 0.0), w=["EXU%d" % i])

                    def first(j, tbase=tbase, tiles=tiles):
                        tw = tiles[j][1]
                        CUR_TW[0] = tw
                        hs = (tbase + j) % 2
                        h = hb[hs]
                        hk = "h%d" % hs
                        Yj = YB[j % 2]
                        yk = "Y%d" % (j % 2)

                        def proj(chunk):
                            b = ps_alloc()
                            for k in range(KC):
                                A("tensor", R.matmul(
                                    PS[b][:], lhsT=WI[:, k, chunk * 128:(chunk + 1) * 128], rhs=h[:, k, :],
                                    start=(k == 0), stop=(k == KC - 1)), r=["WI_%d" % chunk, hk], w=["ps%d" % b])
                            return b

                        def a_p(i):
                            bua = proj(8 + i)
                            f, fk = ft()
                            A("scalar", R.copy(out=f[:], in_=PS[bua][:]), r=["ps%d" % bua], w=[fk])
                            bgc = proj(4 + i)
                            A("vector", R.tensor_tensor(
                                out=EXT[i][:, 30:30 + tw], in0=PS[bgc][:], in1=f[:], op=ALU.mult),
                              r=["ps%d" % bgc, fk], w=["EXT%d" % i])
                            bgb = proj(i)
                            A("scalar", R.copy(out=EXG[i][:, 30:30 + tw], in_=PS[bgb][:]),
                              r=["ps%d" % bgb], w=["EXG%d" % i])

                        def a_c(i):
                            bc = ps_alloc()
                            conv_pe(bc, DGA, i, 3, EXT[i], "EXT%d" % i, 14, "DGA")
                            A("vector", R.tensor_tensor(
                                out=Yj[:, i, :], in0=PS[bc][:], in1=EXG[i][:, 15:15 + tw], op=ALU.mult),
                              r=["ps%d" % bc, "EXG%d" % i], w=[yk])
                            A("gpsimd", R.tensor_copy(out=EXT[i][:, 0:30], in_=EXT[i][:, tw:tw + 30]),
                              r=["EXT%d" % i], w=["EXT%d" % i])
                            A("gpsimd", R.tensor_copy(out=EXG[i][:, 0:30], in_=EXG[i][:, tw:tw + 30]),
                              r=["EXG%d" % i], w=["EXG%d" % i])

                        def b_p(i):
                            bg = proj(16 + i)
                            f, fk = ft()
                            A("scalar", R.activation(out=f[:], in_=PS[bg][:], func=AF.Sigmoid),
                              r=["ps%d" % bg], w=[fk])
                            bv = proj(12 + i)
                            A("vector", R.tensor_tensor(
                                out=EXU[i][:, 30:30 + tw], in0=PS[bv][:], in1=f[:], op=ALU.mult),
                              r=["ps%d" % bv, fk], w=["EXU%d" % i])

                        def b_c(i):
                            bc = ps_alloc()
                            conv_pe(bc, DGB, i, 31, EXU[i], "EXU%d" % i, 0, "DGB")
                            A("gpsimd", R.tensor_copy(out=EXU[i][:, 0:30], in_=EXU[i][:, tw:tw + 30]),
                              r=["EXU%d" % i], w=["EXU%d" % i])
                            A("scalar", R.activation(
                                out=U2[i][:], in_=PS[bc][:], func=AF.Identity, bias=ppc(l, O_CBB + i), scale=1.0),
                              r=["ps%d" % bc, "PP"], w=["U2_%d" % i])
                            A("scalar", R.activation(
                                out=SQB[i % 2][:], in_=PS[bc][:], func=AF.Square, bias=ppc(l, O_CBB + i), scale=1.0),
                              r=["ps%d" % bc, "PP"], w=["SQB%d" % (i % 2)])
                            A("gpsimd", R.tensor_copy(out=U2B[i % 2][:], in_=U2[i][:]),
                              r=["U2_%d" % i], w=["U2B%d" % (i % 2)])

                        def b_s(i):
                            A("tensor", R.matmul(PS[6][:], lhsT=ONESB[:], rhs=U2B[i % 2][:], start=(i == 0), stop=(i == 3)),
                              r=["U2B%d" % (i % 2), "ONESB"], w=["ps6"])
                            A("tensor", R.matmul(PS[7][:], lhsT=ONESB[:], rhs=SQB[i % 2][:], start=(i == 0), stop=(i == 3)),
                              r=["SQB%d" % (i % 2), "ONESB"], w=["ps7"])
                        a_p(0)
                        a_p(1)
                        a_c(0)
                        a_p(2)
                        a_c(1)
                        a_p(3)
                        a_c(2)
                        b_p(0)
                        a_c(3)
                        b_p(1)
                        b_c(0)
                        b_p(2)
                        b_c(1)
                        b_s(0)
                        b_p(3)
                        b_c(2)
                        b_s(1)
                        b_c(3)
                        b_s(2)
                        b_s(3)
                        f, fk = ft()
                        A("vector", R.tensor_scalar(out=PS[6][:], in0=PS[6][:], scalar1=1.0 / 512, scalar2=None, op0=ALU.mult),
                          r=["ps6"], w=["ps6"])
                        A("scalar", R.activation(out=f[:], in_=PS[6][:], func=AF.Square), r=["ps6"], w=[fk])
                        A("vector", R.scalar_tensor_tensor(
                            out=f[:], in0=PS[7][:], scalar=1.0 / 512, in1=f[:], op0=ALU.mult, op1=ALU.subtract),
                          r=["ps7", fk], w=[fk])
                        A("scalar", R.activation(out=f[:], in_=f[:], func=AF.Ln, scale=1.0, bias=EPSB[:, 0:1]),
                          r=[fk], w=[fk])
                        A("scalar", R.activation(out=PS[7][:], in_=f[:], func=AF.Exp, scale=-0.5), r=[fk], w=["ps7"])
                        for i in range(4):
                            f, fk = ft()
                            A("vector", R.tensor_tensor(out=f[:], in0=U2[i][:], in1=PS[6][:], op=ALU.subtract),
                              r=["U2_%d" % i, "ps6"], w=[fk])
                            A("vector", R.tensor_tensor(out=f[:], in0=f[:], in1=PS[7][:], op=ALU.mult),
                              r=[fk, "ps7"], w=[fk])
                            A("scalar", R.activation(
                                out=Yj[:, 4 + i, :], in_=f[:], func=AF.Silu, scale=ppc(l, O_LG + i), bias=ppc(l, O_LB + i)),
                              r=[fk, "PP"], w=[yk])

                    def second(j, v=v, SRC=SRC, DST=DST, ntile=ntile, tiles=tiles):
                        a0 = tiles[j][0]
                        CUR_TW[0] = tiles[j][1]
                        residual_out(l, 2, v, WO, "WO", YB[j % 2], "Y%d" % (j % 2), KC, DST, PADL + a0 - 15)
                        if j + 1 < ntile:
                            load_x(SRC, PADL + tiles[j + 1][0] - 15, key="xr", dst=xr, tw=tiles[j + 1][1])
                    load_x(SRC, PADL - 15, key="xr", dst=xr, tw=tiles[0][1])
                    pre(0)
                    first(0)
                    if ntile > 1:
                        pre(1)
                    for j in range(ntile):
                        if j + 1 < ntile:
                            first(j + 1)
                        if j + 2 < ntile:
                            pre(j + 2)
                        second(j)
                    tix += ntile
                S.fence()
        else:
            o_ = l // 2
            with contextlib.ExitStack() as ph:
                uid[0] += 1
                sb = lambda name, shape, dt, u=uid[0]: ph.enter_context(nc.sbuf_tensor("%s_%d" % (name, u), shape, dt))
                WQ = sb("WQ", [128, KC, NQ], BF16)
                WO = sb("WO", [128, KC, D], BF16)
                QR = sb("QR", [128, KC, 8 * 128], BF16)
                KR = sb("KR", [128, 4, 8 * 128], BF16)
                VR = sb("VR", [128, 8, 4, 128], BF16)
                QC = sb("QC", [128, KC, NCTX], BF16)
                KCX = sb("KCX", [128, 4, NCTX], BF16)
                VCX = sb("VCX", [128, 4, 4, 128], BF16)
                ROPE = _W(sb("ROPE", [128, 2, TW], F32))
                EX = [sb("EX%d" % i, [128, TW], BF16) for i in range(4)]
                AOT = sb("AO", [128, KC, TW], BF16)
                AO = AOT
                AOW = _W(AOT)
                ESK = sb("ESK", [128, 16], F32)
                RD = [sb("RD%d" % i, [64, TW], F32) for i in range(2)]
                RT = [_W(sb("RT%d" % i, [128, TW], F32)) for i in range(4)]
                load_wc(WQ, "WQ", w_qkv[o_], KC, [(c, c, 256) for c in (list(range(2048, 3072, 256)) + list(range(0, 2048, 256)) + [3072])])
                load_wc(WO, "WO", w_o[o_], KC, [(c, c, 256) for c in range(0, D, 256)])
                A("gpsimd", R.memset(VR[:], 1.0), w=["VR%d" % s for s in range(8)])
                A("gpsimd", R.memset(VCX[:], 1.0), w=["VCX"])
                A("gpsimd", R.memset(AO[:], 0.0), w=["AO"])
                A("scalar", R.activation(out=ESK[:], in_=PP[:, l, O_SNK:O_SNK + 16], func=AF.Exp), r=["PP"], w=["ESK"], small=True)
                HORD = [0, 2, 1, 3]
                rt_rr = [0]
                ex_rr = [0]
                pv_rr = [0]
                pend_pv = [None]

                def attend(qsrc, qkey, qcol, kchunks, aocol):
                    CUR_TW[0] = TW
                    nkc = len(kchunks)
                    items = [(g, ci) for g in range(4) for ci in range(nkc)]
                    pvbank = {}
                    for g in range(4):
                        pvbank[g] = 6 + pv_rr[0]
                        pv_rr[0] ^= 1

                    def s_stage(g, ci):
                        (kget, kkey, vget, vkey, mi) = kchunks[ci]
                        ei = ex_rr[0]
                        ex_rr[0] = (ei + 1) % 4
                        for half in range(2):
                            b = ps_alloc()
                            A("tensor", R.matmul(
                                PS[b][:, 0:256],
                                lhsT=kget(g)[half * 64:(half + 1) * 64, :],
                                rhs=qsrc[half * 64:(half + 1) * 64, 2 * g:2 * g + 2, qcol:qcol + 128],
                                start=True, stop=True), r=[kkey, qkey], w=["ps%d" % b])
                            A("scalar", R.activation(out=EX[ei][:, half * 256:(half + 1) * 256], in_=PS[b][:, 0:256], func=AF.Exp, scale=0.125),
                              r=["ps%d" % b], w=["EX%d" % ei])
                        if mi is not None:
                            A("vector", R.tensor_tensor(out=EX[ei][:], in0=EX[ei][:], in1=MASK[:, mi, :], op=ALU.mult),
                              r=["EX%d" % ei, "MASK"], w=["EX%d" % ei])
                        return ei

                    def pv_stage(g, ci, ei):
                        (kget, kkey, vget, vkey, mi) = kchunks[ci]
                        pvb = pvbank[g]
                        A("tensor", R.matmul(
                            PS[pvb][:], lhsT=vget(g), rhs=EX[ei][:], start=(ci == 0), stop=(ci == nkc - 1)),
                          r=["EX%d" % ei, vkey], w=["ps%d" % pvb])
                        if ci != nkc - 1:
                            return
                        rd = RD[g % 2]
                        rk = "RD%d" % (g % 2)
                        for cb in range(4):
                            hd = g * 4 + HORD[cb]
                            A("scalar", R.activation(
                                out=rd[:, cb * 128:(cb + 1) * 128], in_=PS[pvb][64:128, cb * 128:(cb + 1) * 128],
                                func=AF.Ln, bias=ESK[64:128, hd:hd + 1], scale=1.0),
                              r=["ps%d" % pvb, "ESK"], w=[rk])
                        A("scalar", R.activation(out=rd[:], in_=rd[:], func=AF.Exp, scale=-1.0), r=[rk], w=[rk])
                        for half in range(2):
                            A("vector", R.tensor_tensor(
                                out=AO[half * 64:(half + 1) * 64, 2 * g:2 * g + 2, aocol:aocol + 128],
                                in0=PS[pvb][0:64, half * 256:(half + 1) * 256].rearrange("p (a t) -> p a t", a=2),
                                in1=rd[:, half * 256:(half + 1) * 256].rearrange("p (a t) -> p a t", a=2), op=ALU.mult),
                              r=["ps%d" % pvb, rk], w=["AO"])
                    for (g, ci) in items:
                        ei = s_stage(g, ci)
                        if pend_pv[0] is not None:
                            pend_pv[0]()
                        pend_pv[0] = (lambda g=g, ci=ci, ei=ei: pv_stage(g, ci, ei))

                tix = 0
                for (sname, v) in ([("ctx", 1), ("main", 0)]):
                    SRC, DST = (CB[0], CB[1]) if v else (XB[0], XB[1])
                    tiles = geom(v, False)
                    ntile = len(tiles)
                    tbase = tix

                    def pre(j, v=v, SRC=SRC, tbase=tbase, ntile=ntile, tiles=tiles):
                        CUR_TW[0] = tiles[j][1]
                        if j == 0:
                            load_x(SRC, PADL)
                        norm_mod(l, 1, 0, v, (tbase + j) % 2, nvalid=(NCTX if v else TW))
                        if j + 1 < ntile:
                            load_x(SRC, PADL + tiles[j + 1][0], tw=tiles[j + 1][1])
                    pre(0)
                    for j in range(ntile):
                        a0, tw = tiles[j]
                        CUR_TW[0] = tw
                        nblk = tw // 128
                        hs = tix % 2
                        tix += 1
                        h = hb[hs]
                        hk = "h%d" % hs
                        lag = 0 if v else 128
                        if not (v and last):
                            load_x(SRC, PADL + a0 - lag, key="xr", dst=xr)
                        rope = ROPE
                        rkey = "ROPE"
                        if (not v) and j == 0:
                            A("sync", R.dma_start(out=ROPE[:], in_=rope_in[:, :, a0:a0 + tw].rearrange("a p t -> p a t")),
                              w=["ROPE"], dma=True)
                        for m in (8, 9, 10, 11, 0, 1, 2, 3, 4, 5, 6, 7):
                            if m < 8:
                                c0, c1 = m * 128, 1024 + m * 128
                            else:
                                c0, c1 = 2048 + (m - 8) * 128, 2560 + (m - 8) * 128
                            if v and last and m < 8:
                                continue
                            if v:
                                bq = ps_alloc()
                                for k in range(KC):
                                    A("tensor", R.matmul(
                                        PS[bq][:], lhsT=WQ[:, k, c0:c0 + 128], rhs=h[:, k, :], start=(k == 0), stop=(k == KC - 1)),
                                      r=["WQ_%d" % (c0 // 128), hk], w=["ps%d" % bq])
                                dst_ap, dk_ = (QC[:, m, :], "QC") if m < 8 else (KCX[:, m - 8, :], "KCX")
                                A("scalar", R.copy(out=dst_ap, in_=PS[bq][:, 0:NCTX]), r=["ps%d" % bq], w=[dk_])
                                continue
                            bq = ps_alloc()
                            br = ps_alloc()
                            for (b, cc) in ((bq, c0), (br, c1)):
                                for k in range(KC):
                                    A("tensor", R.matmul(
                                        PS[b][:], lhsT=WQ[:, k, cc:cc + 128], rhs=h[:, k, :], start=(k == 0), stop=(k == KC - 1)),
                                      r=["WQ_%d" % (cc // 128), hk], w=["ps%d" % b])
                            i1 = rt_rr[0]
                            i2 = (i1 + 1) % 4
                            rt_rr[0] = (i1 + 2) % 4
                            A("vector", R.tensor_tensor(out=RT[i1][:], in0=PS[bq][:], in1=rope[:, 0, :], op=ALU.mult),
                              r=["ps%d" % bq, rkey], w=["RT%d" % i1])
                            A("vector", R.tensor_tensor(out=RT[i2][:], in0=PS[br][:], in1=rope[:, 1, :], op=ALU.mult),
                              r=["ps%d" % br, rkey], w=["RT%d" % i2])
                            if v:
                                if m < 8:
                                    dst_ap, dkeys = QC[:, m, :], ["QC"]
                                else:
                                    dst_ap, dkeys = KCX[:, m - 8, :], ["KCX"]
                            else:
                                sl0 = (4 * j) % 8
                                if m < 8:
                                    dst_ap, dkeys = QR[:, m, sl0 * 128:sl0 * 128 + tw], ["QR%d" % (sl0 + s) for s in range(nblk)]
                                else:
                                    dst_ap, dkeys = KR[:, m - 8, sl0 * 128:sl0 * 128 + tw], ["KR%d" % (sl0 + s) for s in range(nblk)]
                            A("gpsimd", R.tensor_tensor(out=dst_ap, in0=RT[i1][:], in1=RT[i2][:], op=ALU.add),
                              r=["RT%d" % i1, "RT%d" % i2], w=dkeys)
                        if (not v) and j + 1 < ntile:
                            n0, ntw = tiles[j + 1]
                            A("sync", R.dma_start(out=ROPE.t[:, :, 0:ntw], in_=rope_in[:, :, n0:n0 + ntw].rearrange("a p t -> p a t")),
                              w=["ROPE"], dma=True)
                        for blk in range(nblk):
                            b = ps_alloc()
                            for k in range(KC):
                                A("tensor", R.matmul(
                                    PS[b][:, 0:256], lhsT=h[:, k, blk * 128:(blk + 1) * 128], rhs=WQ[:, k, 3072:3328],
                                    start=(k == 0), stop=(k == KC - 1)), r=["WQ_24", "WQ_25", hk], w=["ps%d" % b])
                            if v:
                                dst_ap, dk = VCX[:, blk, :, 0:64], "VCX"
                            else:
                                sl = (4 * j + blk) % 8
                                dst_ap, dk = VR[:, sl, :, 0:64], "VR%d" % sl
                            A("scalar", R.copy(out=dst_ap, in_=PS[b][:, 0:256].rearrange("p (g d) -> p g d", g=4)),
                              r=["ps%d" % b], w=[dk])
                        if j + 1 < ntile:
                            pre(j + 1)
                        if v and last:
                            continue
                        ctxch = [((lambda g, cc=cc: KCX[:, g, cc * 128:(cc + 1) * 128]), "KCX",
                                  (lambda g, cc=cc: VCX[:, cc, g, :]), "VCX", None) for cc in range(2)]
                        if v:
                            for qb in range(2):
                                attend(QC, "QC", qb * 128, ctxch, qb * 128)
                        else:
                            for qi in range(nblk):
                                qb = 4 * j - 1 + qi
                                if qb < 0:
                                    continue
                                chs = []
                                for (kb, mi) in ((qb - 1, 0), (qb, None), (qb + 1, 1)):
                                    if kb < 0:
                                        continue
                                    sl = kb % 8
                                    chs.append(((lambda g, sl=sl: KR[:, g, sl * 128:(sl + 1) * 128]), "KR%d" % sl,
                                                (lambda g, sl=sl: VR[:, sl, g, :]), "VR%d" % sl, mi))
                                sq_ = qb % 8
                                attend(QR, "QR%d" % sq_, sq_ * 128, chs + ctxch, qi * 128)
                        if pend_pv[0] is not None:
                            pend_pv[0]()
                            pend_pv[0] = None
                        CUR_TW[0] = tw
                        residual_out(l, 2, v, WO, "WO", AOW, "AO", KC, DST, PADL + a0 - lag)
                S.fence()
        if stop_after == ("m", l):
            break
        for hf in range(2):
            with contextlib.ExitStack() as ph:
                uid[0] += 1
                sb = lambda name, shape, dt, u=uid[0]: ph.enter_context(nc.sbuf_tensor("%s_%d" % (name, u), shape, dt))
                WU = sb("WU", [128, KC, 2816], BF16)
                WD = sb("WD", [128, 11, D], BF16)
                DG = sb("DG", [128, 66, 128], BF16)
                EXF = [sb("EXF%d" % i, [128, 2 + TW], BF16) for i in range(4)]
                CAR = sb("CAR", [128, 22, 2], BF16)
                SG = [_W(sb("SG%d" % i, [128, TW], F32)) for i in range(2)]
                ACT = _W(sb("ACT", [128, 11, TW], BF16))
                ACT2 = _W(sb("ACT2", [128, 11, TW], BF16))
                ublk = []
                for c in range(0, 1408, 256):
                    n = min(256, 1408 - c)
                    ublk.append((c, hf * 1408 + c, n))
                    ublk.append((1408 + c, 2816 + hf * 1408 + c, n))
                load_wc(WU, "WU", w_up[l], KC, ublk)
                load_wc(WD, "WD", w_dn[l, hf * 1408:(hf + 1) * 1408, :], 11, [(c, c, 256) for c in range(0, D, 256)])
                for ci in [x for pj in range(11) for x in (pj, 11 + pj)]:
                    gch = (hf * 11 + ci) if ci < 11 else (22 + hf * 11 + ci - 11)
                    for k in range(3):
                        A("vector", R.tensor_scalar(
                            out=DG[:, ci * 3 + k, :], in0=IDN[:], scalar1=ppc(l, O_WCF + gch * 3 + k), scalar2=None, op0=ALU.mult),
                          r=["IDN", "PP"], w=["DG_%d" % ci])
                ACTB = [ACT, ACT2]
                tix = 0
                ex_rr = [0]
                pp_rr = [0]
                cp_rr = [0]

                def palloc():
                    b = 1 + pp_rr[0]
                    pp_rr[0] = (pp_rr[0] + 1) % 3
                    return b

                def calloc():
                    b = 4 + cp_rr[0]
                    cp_rr[0] = (cp_rr[0] + 1) % 4
                    return b
                DEPTHP = 2
                for (sname, v) in streams(l, last):
                    Bn, Br, Bd = (CB if v else XB)[1], (CB if v else XB)[1 + hf], (CB if v else XB)[(2 + hf) % 3]
                    tiles = geom(v, True)
                    ntile = len(tiles)
                    A("gpsimd", R.memset(CAR[:], 0.0), w=["CAR"])
                    tbase = tix

                    def pre(j, v=v, Bn=Bn, tbase=tbase, ntile=ntile, tiles=tiles):
                        CUR_TW[0] = tiles[j][1]
                        if j == 0:
                            load_x(Bn, PADL)
                        norm_mod(l, 4, 3, v, (tbase + j) % 2, nvalid=(NCTX if v else TW))
                        if j + 1 < ntile:
                            load_x(Bn, PADL + tiles[j + 1][0], tw=tiles[j + 1][1])

                    def first(j, tbase=tbase, tiles=tiles):
                        tw = tiles[j][1]
                        CUR_TW[0] = tw
                        hs = (tbase + j) % 2
                        h = hb[hs]
                        hk = "h%d" % hs
                        actb = ACTB[j % 2]
                        ak = "ACT%d" % (j % 2)
                        pend = []
                        convb = {}

                        def tail(item):
                            ci, nm, pj, ei = item
                            ek = "EXF%d" % ei
                            bc = calloc()
                            conv_pe(bc, DG, ci, 3, EXF[ei], ek, 0, "DG")
                            convb[(nm, pj)] = bc
                            if nm == "g":
                                sg = SG[pj % 2]
                                sk = "SG%d" % (pj % 2)
                                bg, ba = convb.pop(("g", pj)), convb.pop(("a", pj))
                                A("scalar", R.activation(out=sg[:], in_=PS[bg][:], func=AF.Silu), r=["ps%d" % bg], w=[sk])
                                A("vector", R.tensor_tensor(out=actb[:, pj, :], in0=PS[ba][:], in1=sg[:], op=ALU.mult),
                                  r=["ps%d" % ba, sk], w=[ak])
                        for pj in range(11):
                            for (ci, nm) in ((pj, "a"), (11 + pj, "g")):
                                b = palloc()
                                c0 = ci * 128
                                for k in range(KC):
                                    A("tensor", R.matmul(
                                        PS[b][:], lhsT=WU[:, k, c0:c0 + 128], rhs=h[:, k, :], start=(k == 0), stop=(k == KC - 1)),
                                      r=["WU_%d" % ci, hk], w=["ps%d" % b])
                                ei = ex_rr[0]
                                ex_rr[0] = (ei + 1) % 4
                                ek = "EXF%d" % ei
                                A("gpsimd", R.tensor_copy(out=EXF[ei][:, 0:2], in_=CAR[:, ci, :]), r=["CAR"], w=[ek])
                                A("scalar", R.copy(out=EXF[ei][:, 2:2 + tw], in_=PS[b][:]), r=["ps%d" % b], w=[ek])
                                A("gpsimd", R.tensor_copy(out=CAR[:, ci, :], in_=EXF[ei][:, tw:tw + 2]), r=[ek], w=["CAR"])
                                pend.append((ci, nm, pj, ei))
                                if len(pend) > DEPTHP:
                                    tail(pend.pop(0))
                        while pend:
                            tail(pend.pop(0))

                    def second(j, v=v, Br=Br, Bd=Bd, ntile=ntile, tiles=tiles):
                        a0 = tiles[j][0]
                        CUR_TW[0] = tiles[j][1]
                        residual_out(l, 5, v, WD, "WD", ACTB[j % 2], "ACT%d" % (j % 2), 11, Bd, PADL + a0 - 1, alloc=palloc)
                        if j + 1 < ntile:
                            load_x(Br, PADL + tiles[j + 1][0] - 1, key="xr", dst=xr, tw=tiles[j + 1][1])
                    load_x(Br, PADL - 1, key="xr", dst=xr, tw=tiles[0][1])
                    pre(0)
                    first(0)
                    if ntile > 1:
                        pre(1)
                    for j in range(ntile):
                        if j + 1 < ntile:
                            first(j + 1)
                        if j + 2 < ntile:
                            pre(j + 2)
                        second(j)
                    tix += ntile
                S.fence()
        if stop_after == ("f", l):
            break

    finals = []
    CUR_TW[0] = TW
    if stop_after is None:
        for j in range(HALF // TW):
            a0 = j * TW
            load_x(XB[0], PADL + a0)
            norm_mod(0, 0, 0, 0, 0, gain_ap=lambda c: FN[:, c:c + 1], out_f32=xr)
            for c in range(KC):
                finals.append(A("sync", R.dma_start(out=out[c * 128:(c + 1) * 128, a0:a0 + TW], in_=xr[:, c, :]),
                                r=["xr%d" % c], w=["dram_out"], dma=True))
    else:
        finals = [d for d in S.dma_last if d is not None]
    S.emit(final_wait_ops=finals)
    outer.close()
    return nc


def _fm(vec):
    v = np.asarray(vec, np.float32).reshape(-1, 8, 128)
    return np.ascontiguousarray(v.transpose(2, 0, 1).reshape(128, -1))


def _prep_common(inp):
    w_qkv = np.asarray(inp["w_qkv"], np.float32)
    part = np.concatenate([np.arange(32, 64), np.arange(0, 32)])
    qcols = np.arange(1024)
    qrot = (np.arange(16)[:, None] * 64 + part[None, :]).reshape(-1)
    kd, kdr = [], []
    for g in range(4):
        base = 1024 + g * 64
        kd += [base + np.arange(64), base + np.arange(64)]
        kdr += [base + part, base + part]
    cols = np.concatenate([qcols, qrot, np.concatenate(kd), np.concatenate(kdr), 1280 + np.arange(256)])
    w_qkv_ext = np.ascontiguousarray(w_qkv[:, :, cols])
    tri = np.tril(np.ones((128, 128), np.float32))
    maskP = np.tile(tri, (1, 4))
    maskN = np.tile(tri.T, (1, 4))
    masks = np.ascontiguousarray(np.concatenate([maskP, maskN], axis=1))
    return dict(
        w_mod=np.ascontiguousarray(inp["w_mod"], np.float32), w_in_ab=np.ascontiguousarray(inp["w_in_ab"], np.float32),
        w_out_ab=np.ascontiguousarray(inp["w_out_ab"], np.float32), w_qkv_ext=w_qkv_ext,
        w_o=np.ascontiguousarray(inp["w_o"], np.float32), w_up=np.ascontiguousarray(inp["w_up"], np.float32),
        w_down=np.ascontiguousarray(inp["w_down"], np.float32), ident=np.eye(128, dtype=np.float32), masks=masks,
        fnorm=_fm(inp["final_norm"]))


def _prep_core(inp, core):
    b, half = core // 2, core % 2
    x = np.asarray(inp["x"], np.float32)[b]
    ctx = np.asarray(inp["ctx"], np.float32)[b]
    if half:
        x = x[::-1]
        ctx = ctx[::-1]
    x_fm = np.zeros((D, XW), np.float32)
    x_fm[:, PADL:] = x[:T].T
    c_fm = np.zeros((D, CW), np.float32)
    c_fm[:, PADL:PADL + NCTX] = ctx.T
    cv = np.stack([np.asarray(inp["c"], np.float32)[b], np.asarray(inp["c_ctx"], np.float32)], 0)
    cvec = np.ascontiguousarray(cv.reshape(2, 8, 128).transpose(2, 1, 0).reshape(128, 16))
    flip = (lambda a: a[::-1]) if half else (lambda a: a)
    pp = np.zeros((128, DEPTH, NPP), np.float32)
    for l in range(DEPTH):
        pp[:, l, O_NM:O_NM + 8] = _fm(inp["norm_mix"][l])
        pp[:, l, O_NF:O_NF + 8] = _fm(inp["norm_ffn"][l])
        pp[:, l, O_BM:O_BM + 48] = _fm(inp["b_mod"][l])
        if l % 2 == 0:
            e = l // 2
            ca = flip(np.asarray(inp["conv_a"], np.float32)[e])
            pp[:, l, O_CA:O_CA + 12] = ca.reshape(3, 4, 128).transpose(2, 1, 0).reshape(128, 12)
            cbw = flip(np.asarray(inp["conv_b"], np.float32)[e])
            pp[:, l, O_CB:O_CB + 124] = cbw.reshape(31, 4, 128).transpose(2, 1, 0).reshape(128, 124)
            pp[:, l, O_CBB:O_CBB + 4] = np.asarray(inp["conv_b_bias"], np.float32)[e].reshape(4, 128).T
            pp[:, l, O_LG:O_LG + 4] = np.asarray(inp["ln_b_gain"], np.float32)[e].reshape(4, 128).T
            pp[:, l, O_LB:O_LB + 4] = np.asarray(inp["ln_b_bias"], np.float32)[e].reshape(4, 128).T
        else:
            pp[:, l, O_SNK:O_SNK + 16] = np.asarray(inp["sinks"], np.float32)[l // 2][None, :]
        wc = flip(np.asarray(inp["w_conv_ffn"], np.float32)[l])
        pp[:, l, O_WCF:O_WCF + 132] = wc.reshape(3, 44, 128).transpose(2, 1, 0).reshape(128, 132)
    j = np.arange(T)
    pos = (SEQ - 1 - j) if half else j
    row = (pos // 64).astype(np.float32)
    col = (pos % 64).astype(np.float32)
    inv = (np.float32(10000.0) ** (-np.arange(16, dtype=np.float32) / np.float32(16))).astype(np.float32)
    ang = np.concatenate([row[:, None] * inv[None, :], col[:, None] * inv[None, :]], axis=1).astype(np.float32)
    cs, sn = np.cos(ang).astype(np.float32), np.sin(ang).astype(np.float32)
    cos64 = np.concatenate([cs, cs], axis=1).T
    sin64 = np.concatenate([-sn, sn], axis=1).T
    rope = np.stack([np.concatenate([cos64, cos64], 0), np.concatenate([sin64, sin64], 0)], 0)
    return dict(x_fm=x_fm, ctx_fm=c_fm, cvec=cvec, pp=np.ascontiguousarray(pp.reshape(128, DEPTH * NPP)),
                rope=np.ascontiguousarray(rope, np.float32))


_NC_CACHE = {}


def kernel(**inputs):
    common = _prep_common(inputs)
    in_maps = []
    for core in range(8):
        m = dict(common)
        m.update(_prep_core(inputs, core))
        in_maps.append(m)
    if "nc" not in _NC_CACHE:
        _NC_CACHE["nc"] = build_program()
    res = run_bass_kernel_spmd(_NC_CACHE["nc"], in_maps, core_ids=list(range(8)))
    outp = np.empty((4, SEQ, D), np.float32)
    for core in range(8):
        b, half = core // 2, core % 2
        o = res.results[core]["out_fm"].T
        if half:
            outp[b, HALF:] = o[::-1]
        else:
            outp[b, :HALF] = o
    return outp
```
